# Optimizing a Trainium2 kernel written in Bass

```python
import jax, jax.numpy as jnp
from jax import lax
import numpy as np

D_MODEL = 2048
BATCH = 16
SEQ = 2048
DEPTH = 1
DEC_BATCH = 16
DEC_SEQ = 32
PAST_LEN = 1024

CHUNK = 64
CONV_WIDTH = 3
D_CONV = D_MODEL
N_CONV_GROUPS = 32
N_RET_HEADS = 8
RET_DK = D_MODEL // 16
RET_DV = D_MODEL // 8
D_RET_QK = N_RET_HEADS * RET_DK
D_RET_V = N_RET_HEADS * RET_DV
D_BRANCH = D_MODEL
ROPE_BASE = 10000.0
EPS = 1e-6
IN_SIZES = (D_CONV, D_CONV, D_CONV, D_CONV, D_RET_QK, D_RET_QK, D_RET_V, D_RET_V, D_MODEL, D_MODEL)
D_IN_TOTAL = sum(IN_SIZES)
SPLIT_POINTS = tuple(int(s) for s in np.cumsum(IN_SIZES)[:-1])

kernel_name = "hybrid_stream_conv_retention_step"


def rms_norm(x, g):
    xf = x.astype(jnp.float32)
    y = xf * lax.rsqrt(jnp.mean(xf * xf, axis=-1, keepdims=True) + EPS)
    return (y * g.astype(jnp.float32)).astype(x.dtype)


def rope(x, pos):
    d = x.shape[-1]
    inv = ROPE_BASE ** (-jnp.arange(0, d, 2, dtype=jnp.float32) / d)
    ang = pos.astype(jnp.float32)[:, None] * inv[None, :]
    cos = jnp.cos(ang)[None, :, None, :]
    sin = jnp.sin(ang)[None, :, None, :]
    x1, x2 = x[..., : d // 2], x[..., d // 2:]
    return jnp.concatenate([x1 * cos - x2 * sin, x1 * sin + x2 * cos], axis=-1)


def causal_conv(u, hist, w, b):
    T = u.shape[1]
    full = jnp.concatenate([hist.astype(u.dtype), u], axis=1)
    y = b + sum(w[j] * full[:, j:j + T] for j in range(CONV_WIDTH))
    return y, full[:, -(CONV_WIDTH - 1):]


def retention_block(state, qb, kb, vb, lg):
    L = qb.shape[2]
    idx = jnp.arange(L, dtype=jnp.float32)
    intra = jnp.exp(lg[:, None, None] * jnp.abs(idx[:, None] - idx[None, :]))
    q_dec = jnp.exp(lg[:, None] * (idx + 1.0))[..., None]
    k_dec = jnp.exp(lg[:, None] * (L - 1.0 - idx))[..., None]
    blk_dec = jnp.exp(lg * L)[:, None, None]
    scores = jnp.einsum('bhid,bhjd->bhij', qb, kb) * intra
    o = (jnp.einsum('bhij,bhje->bhie', scores, vb)
         + q_dec * jnp.einsum('bhid,bhde->bhie', qb, state))
    new_state = blk_dec * state + jnp.einsum('bhjd,bhje->bhde', kb * k_dec, vb)
    return new_state, o


def retention(q, k, v, state, lg):
    B, H, T, _ = q.shape
    L = min(CHUNK, T)
    n = T // L

    def to_blocks(a):
        return a.reshape(B, H, n, L, a.shape[-1]).transpose(2, 0, 1, 3, 4)

    final, o = lax.scan(lambda s, blk: retention_block(s, blk[0], blk[1], blk[2], lg),
                        state, (to_blocks(q), to_blocks(k), to_blocks(v)))
    o = o.transpose(1, 2, 0, 3, 4).reshape(B, H, T, v.shape[-1])
    return o, final


def mixer_layer(x, c, conv_hist, ret_state, pos, ada_w, ada_b, g_pre, g_post,
                w_in, conv_w, conv_b, w_branch, w_out):
    B, T, _ = x.shape
    f32 = jnp.float32
    mod = jax.nn.silu(c) @ ada_w + ada_b
    shift, scale, gate = jnp.split(mod, 3, axis=-1)
    h = rms_norm(x, g_pre) * (1.0 + scale[:, None]) + shift[:, None]
    proj = h @ w_in
    gb, gc, gu, zc, q, k, v, zr, ma, mb = jnp.split(proj, SPLIT_POINTS, axis=-1)

    conv_out, new_hist = causal_conv(gc * gu, conv_hist, conv_w, conv_b)
    y_conv = gb * conv_out * jax.nn.silu(zc)

    lg = jnp.log(1.0 - 2.0 ** (-5.0 - jnp.arange(N_RET_HEADS, dtype=f32)))
    qh = rope(q.reshape(B, T, N_RET_HEADS, RET_DK).astype(f32), pos).transpose(0, 2, 1, 3)
    kh = (rope(k.reshape(B, T, N_RET_HEADS, RET_DK).astype(f32), pos) * (RET_DK ** -0.5)).transpose(0, 2, 1, 3)
    vh = v.reshape(B, T, N_RET_HEADS, RET_DV).astype(f32).transpose(0, 2, 1, 3)
    o, new_state = retention(qh, kh, vh, ret_state.astype(f32), lg)
    o = o * lax.rsqrt(jnp.mean(o * o, axis=-1, keepdims=True) + EPS)
    y_ret = o.transpose(0, 2, 1, 3).reshape(B, T, D_RET_V).astype(x.dtype) * jax.nn.silu(zr)

    merged = (jax.nn.sigmoid(ma) * (y_conv @ w_branch[0])
              + jax.nn.sigmoid(mb) * (y_ret @ w_branch[1]))
    out = merged @ w_out
    x = x + gate[:, None] * rms_norm(out, g_post)
    return x, new_hist, new_state.astype(x.dtype)


def setup_inputs(seed: int = 0) -> dict:
    key = jax.random.key(seed)
    ks = jax.random.split(key, 16)
    f32 = jnp.float32
    nrm = lambda k, shape, s: jax.random.normal(k, shape, f32) * s
    return {
        "x_prompt": nrm(ks[0], (BATCH, SEQ, D_MODEL), 1.0),
        "x_sample": nrm(ks[1], (DEC_BATCH, DEC_SEQ, D_MODEL), 1.0),
        "c_prompt": nrm(ks[2], (BATCH, D_MODEL), 1.0),
        "c_sample": nrm(ks[3], (DEC_BATCH, D_MODEL), 1.0),
        "state_conv": nrm(ks[4], (DEPTH, DEC_BATCH, CONV_WIDTH - 1, D_CONV), 1.0),
        "state_ret": nrm(ks[5], (DEPTH, DEC_BATCH, N_RET_HEADS, RET_DK, RET_DV), 1.0),
        "ada_w": nrm(ks[6], (DEPTH, D_MODEL, 3 * D_MODEL), 0.5 * D_MODEL ** -0.5),
        "ada_b": nrm(ks[7], (DEPTH, 3 * D_MODEL), 0.01),
        "norm_pre": 1.0 + nrm(ks[8], (DEPTH, D_MODEL), 0.02),
        "norm_post": 1.0 + nrm(ks[9], (DEPTH, D_MODEL), 0.02),
        "w_in": nrm(ks[10], (DEPTH, D_MODEL, D_IN_TOTAL), D_MODEL ** -0.5),
        "conv_w": nrm(ks[11], (DEPTH, CONV_WIDTH, D_CONV), CONV_WIDTH ** -0.5),
        "conv_b": nrm(ks[12], (DEPTH, D_CONV), 0.01),
        "w_branch": nrm(ks[13], (DEPTH, 2, D_BRANCH, D_MODEL), D_BRANCH ** -0.5),
        "w_out": nrm(ks[14], (DEPTH, D_MODEL, D_MODEL), D_MODEL ** -0.5),
    }


def reference(x_prompt, x_sample, c_prompt, c_sample, state_conv, state_ret,
              ada_w, ada_b, norm_pre, norm_post, w_in, conv_w, conv_b, w_branch, w_out):
    bp, tp = x_prompt.shape[0], x_prompt.shape[1]
    ts = x_sample.shape[1]
    pos_p = jnp.arange(tp)
    pos_s = PAST_LEN + jnp.arange(ts)
    hp, hs = x_prompt, x_sample
    conv_p_l, ret_p_l, conv_s_l, ret_s_l = [], [], [], []
    for l in range(DEPTH):
        hist0 = jnp.zeros((bp, CONV_WIDTH - 1, D_CONV), x_prompt.dtype)
        ret0 = jnp.zeros((bp, N_RET_HEADS, RET_DK, RET_DV), jnp.float32)
        hp, cp, rp = mixer_layer(hp, c_prompt, hist0, ret0, pos_p, ada_w[l], ada_b[l],
                                 norm_pre[l], norm_post[l], w_in[l], conv_w[l], conv_b[l],
                                 w_branch[l], w_out[l])
        hs, cs, rs = mixer_layer(hs, c_sample, state_conv[l], state_ret[l], pos_s, ada_w[l], ada_b[l],
                                 norm_pre[l], norm_post[l], w_in[l], conv_w[l], conv_b[l],
                                 w_branch[l], w_out[l])
        conv_p_l.append(cp)
        ret_p_l.append(rp)
        conv_s_l.append(cs)
        ret_s_l.append(rs)
    new_conv_prompt = jnp.stack(conv_p_l)
    new_ret_prompt = jnp.stack(ret_p_l)
    new_conv_sample = jnp.stack(conv_s_l)
    new_ret_sample = jnp.stack(ret_s_l)
    return (hp, hs, new_conv_prompt, new_ret_prompt, new_conv_sample, new_ret_sample)
```

```python
import numpy as np
from contextlib import ExitStack
import concourse.bass as bass
import concourse.mybir as mybir
from concourse.bass_utils import run_bass_kernel_spmd

F32 = mybir.dt.float32
BF16 = mybir.dt.bfloat16
ALU = mybir.AluOpType
AF = mybir.ActivationFunctionType


class Op:
    __slots__ = ("eng", "fn", "reads", "writes", "joins", "dma", "deps", "sig", "semv", "idx", "waits")

    def __init__(self, eng, fn, reads, writes, joins, dma):
        self.eng = eng
        self.fn = fn
        self.reads = reads
        self.writes = writes
        self.joins = joins
        self.dma = dma
        self.deps = set()
        self.sig = False
        self.semv = None
        self.waits = []


class Prog:
    ENGS = ("pe", "act", "dve", "pool", "sp")

    def __init__(self):
        self.ops = []
        self.wr = {}
        self.rd = {}
        self.prev = {}
        self.group_final = {("dma", "const")}

    def op(self, eng, fn, reads=(), writes=(), joins=(), dma=None):
        o = Op(eng, fn, tuple(reads), tuple(writes), tuple(joins), dma)
        j = len(self.ops)
        o.idx = j
        me = ("dma", dma) if dma is not None else ("eng", eng)

        def add(ek, i, raw):
            if ek == ("eng", "pe") and eng == "pe" and o.dma is None:
                return
            o.deps.add(i)

        for k in o.reads:
            for ek, i in self.wr.get(k, {}).items():
                add(ek, i, True)
        for k in o.writes:
            pv = {}
            for d in (self.wr.get(k, {}), self.rd.get(k, {})):
                for ek, i in d.items():
                    if pv.get(ek, -1) < i:
                        pv[ek] = i
            self.prev[k] = pv
            for ek, i in pv.items():
                add(ek, i, False)
        for k in o.joins:
            for ek, i in self.prev.get(k, {}).items():
                add(ek, i, False)
        for k in o.reads:
            self.rd.setdefault(k, {})[me] = j
        for k in o.writes:
            self.wr[k] = {me: j}
            self.rd[k] = {}
        for k in o.joins:
            self.wr.setdefault(k, {})[me] = j
        self.ops.append(o)
        return o

    def finalize(self):
        ops = self.ops
        for o in ops:
            for i in o.deps:
                ops[i].sig = True
        cnt = {}
        for o in ops:
            if o.dma is not None:
                key = ("dma", o.dma)
                cnt[key] = cnt.get(key, 0) + 16
                o.semv = (key, cnt[key])
            elif o.sig:
                key = ("eng", o.eng)
                cnt[key] = cnt.get(key, 0) + 1
                o.semv = (key, cnt[key])
        self.final_waits = dict((k, v) for k, v in cnt.items() if k[0] == "dma")
        waited = {e: {} for e in self.ENGS}
        for o in ops:
            need = {}
            for i in o.deps:
                k, v = ops[i].semv
                if k in self.group_final:
                    v = cnt[k]
                if v > need.get(k, 0):
                    need[k] = v
            w = waited[o.eng]
            for k, v in need.items():
                if w.get(k, 0) < v:
                    w[k] = v
                    o.waits.append((k, v))
        waited_vals = {}
        for o in ops:
            for k, v in o.waits:
                if k[0] == "dma":
                    waited_vals.setdefault(k, set()).add(v)
        for o in ops:
            if o.dma is not None:
                k, v = o.semv
                s0 = v - 16
                if s0 > 0 and s0 in waited_vals.get(k, ()):
                    w = waited[o.eng]
                    if not any(kk == k and vv >= s0 for kk, vv in o.waits):
                        o.waits.append((k, s0))
        self.sem_keys = sorted(cnt.keys(), key=str)
        return self

    def emit(self, nc, stack, final_eng="sp"):
        sems = {}
        for n, k in enumerate(self.sem_keys):
            sems[k] = stack.enter_context(nc.semaphore("sem%d" % n))
        block = stack.enter_context(nc.Block())
        by_eng = {e: [o for o in self.ops if o.eng == e] for e in self.ENGS}
        final_waits = self.final_waits

        def run(e, name):
            for o in by_eng[name]:
                for k, v in o.waits:
                    e.wait_ge(sems[k], v)
                ins = o.fn(e)
                if o.semv is not None:
                    k, v = o.semv
                    ins.then_inc(sems[k], 16 if o.dma is not None else 1)
            if name == final_eng:
                for k, v in final_waits.items():
                    e.wait_ge(sems[k], v)

        @block.tensor
        def _(e):
            run(e, "pe")

        @block.scalar
        def _(e):
            run(e, "act")

        @block.vector
        def _(e):
            run(e, "dve")

        @block.gpsimd
        def _(e):
            run(e, "pool")

        @block.sync
        def _(e):
            run(e, "sp")


D = 2048
KC = 16
T = 512
H = 8
DK = 128
DV = 256
SEQ = 2048
DEC = 32
PAST = 1024
EPS = 1e-6
NSLOT = 52
N_CORES = 8
EVAC_MODE = 0


def host_consts():
    c = {}
    inv = 10000.0 ** (-np.arange(0, DK, 2, dtype=np.float32) / DK)

    def rope_tabs(pos):
        ang = pos.astype(np.float32)[None, :] * inv[:, None]
        cos = np.cos(ang).astype(np.float32)
        sin = np.sin(ang).astype(np.float32)
        cosT = np.concatenate([cos, cos], 0)
        sinX = np.concatenate([sin, -sin], 0)
        return np.ascontiguousarray(cosT), np.ascontiguousarray(sinX)

    c["cosP"], c["sinP"] = rope_tabs(np.arange(SEQ))
    ps = PAST + np.arange(DEC)
    c["cosS"], c["sinS"] = rope_tabs(np.concatenate([ps, ps]))
    lg = np.log(1.0 - 2.0 ** (-5.0 - np.arange(H, dtype=np.float64)))
    for L in (64, 32):
        p = np.arange(128)
        j = (p % L)[:, None, None].astype(np.float64)
        i = np.arange(L)[None, None, :].astype(np.float64)
        l3 = lg[None, :, None]
        mask = np.exp(l3 * (np.abs(i - j) - (i + 1.0))) * (DK ** -0.5)
        c["mask%d" % L] = mask.astype(np.float32)
        qd = np.exp(l3 * (i + 1.0)) * np.ones((128, 1, 1))
        c["qdec%d" % L] = qd.astype(np.float32)
        kd = np.exp(lg[None, :] * (L - 1.0 - (p % L)[:, None])) * (DK ** -0.5)
        c["kdec%d" % L] = kd.astype(np.float32)
    c["gpow"] = np.stack([np.exp(lg * 64.0), np.exp(lg * 32.0)]).astype(np.float32)
    c["identf"] = np.eye(128, dtype=np.float32)
    seg = np.zeros((128, 3, 128), np.float32)
    seg[:, 0, :] = 1.0
    seg[:, 1, 0:32] = 1.0
    seg[:, 2, 32:64] = 1.0
    c["oneseg"] = seg
    return c


CONST_SHAPES = {
    "cosP": [128, 2048], "sinP": [128, 2048], "cosS": [128, 64], "sinS": [128, 64],
    "mask64": [128, H, 64], "mask32": [128, H, 32], "qdec64": [128, H, 64], "qdec32": [128, H, 32],
    "kdec64": [128, H], "kdec32": [128, H], "identf": [128, 128], "oneseg": [128, 3, 128],
}


class _Stop(Exception):
    pass


def build_program(npt=4, with_sample=True, nseq=2, stage=99, conv_slots=NSLOT):
    hc = host_consts()
    SEQ = T * max(npt, 1)
    gpow = hc["gpow"]
    nc = bass.Bass("TRN2", target_bir_lowering=False)

    def din(name, shape):
        return nc.dram_tensor(name, shape, F32, kind="ExternalInput").ap()

    def dout(name, shape):
        return nc.dram_tensor(name, shape, F32, kind="ExternalOutput").ap()

    xp = din("xp", [2 * SEQ, D])
    xs = din("xs", [2 * DEC, D])
    c4 = din("c4", [4, D])
    sconv = din("sconv", [2, 2, D])
    sret = din("sret", [2, H, DK, DV])
    ada_w = din("ada_w", [D, 3 * D])
    ada_b = din("ada_b", [3 * D])
    norm_pre = din("norm_pre", [D])
    norm_post = din("norm_post", [D])
    w_in = din("w_in", [D, 18432])
    conv_w = din("conv_w", [3, D])
    conv_b = din("conv_b", [D])
    w_br = din("w_branch", [2, D, D])
    w_out = din("w_out", [D, D])
    cst = {k: din("k_" + k, v) for k, v in CONST_SHAPES.items()}
    yp = dout("yp", [2 * SEQ, D])
    ys = dout("ys", [2 * DEC, D])
    ncp = dout("ncp", [2, 2, D])
    nrp = dout("nrp", [2, H, DK, DV])
    ncs = dout("ncs", [2, 2, D])
    nrs = dout("nrs", [2, H, DK, DV])
    wsc = nc.dram_tensor("wsc", [NSLOT, 128, KC, 512], BF16).ap()

    P = Prog()
    with ExitStack() as st:
        def sb(name, shape, dt=F32):
            return st.enter_context(nc.sbuf_tensor(name, shape, dt))

        def ps(name, shape, dt=F32):
            return st.enter_context(nc.psum_tensor(name, shape, dt))

        xt = [sb("xt%d" % i, [128, D]) for i in range(2)]
        hT = sb("hT", [128, KC, T], BF16)
        wr = [sb("wr%d" % i, [128, KC, 512], BF16) for i in range(3)]
        Ybuf = sb("Ybuf", [128, 32 * 512], BF16)
        yconv = Ybuf[:, 0:8192].rearrange("p (k t) -> p k t", k=KC)
        yret = Ybuf[:, 8192:16384].rearrange("p (k t) -> p k t", k=KC)
        outsb = Ybuf[:].bitcast(F32).rearrange("p (b f) -> p b f", b=4)
        yjunk = Ybuf[:, 0:2048]
        merged = sb("merged", [128, KC, T], BF16)
        hT_f32 = hT[:].rearrange("p k t -> p (k t)").bitcast(F32)
        ggrow = hT_f32[:, 0:2048]
        hjunk = hT[:, 8:12, :].rearrange("p k t -> p (k t)")
        cosT = sb("cosT", [128, T])
        sinX = sb("sinX", [128, T])
        gcs = sb("gcs", [128, T])
        ubuf = sb("ubuf", [128, T + 4])
        cbuf = sb("cbuf", [128, T])
        szc = sb("szc", [128, T])
        tcv = sb("tcv", [128, T])
        rs = sb("rs", [128, T])
        ropeA = sb("ropeA", [128, T])
        ropeB = sb("ropeB", [128, T])
        qT = [sb("qT%d" % i, [128, T], BF16) for i in range(2)]
        kT = [sb("kT%d" % i, [128, T], BF16) for i in range(2)]
        ktok = [sb("ktok%d" % i, [128, 4, DK], BF16) for i in range(2)]
        vtok = [sb("vtok%d" % i, [128, 4, DV], BF16) for i in range(2)]
        szr = [sb("szr%d" % i, [128, 2, T]) for i in range(2)]
        STm = sb("STm", [128, 4, 64], BF16)
        osq = sb("osq", [128, 2, T], BF16)
        rms = sb("rms", [128, T])
        rinv = sb("rinv", [128, T])
        onorm = sb("onorm", [128, 2, T])
        S0 = sb("S0", [128, H, DV])
        Sbf0 = sb("Sbf0", [128, H, DV], BF16)
        S1 = xt[0][:].rearrange("p (h e) -> p h e", h=H)
        Sbf1 = xt[1][:].bitcast(BF16)[:, 0:H * DV].rearrange("p (h e) -> p h e", h=H)
        ta, tb_, mA, mB = gcs, szc, tcv, rs
        diag = [sb("diag%d" % i, [128, 128]) for i in range(2)]
        vecs = sb("vecs", [128, 320])
        stg = [sb("stg%d" % i, [128, 128]) for i in range(3)]
        ostg = sb("ostg", [16, 128])
        cT = vecs[:, 0:64].rearrange("p (s k) -> p s k", s=4)
        scT = sb("scT", [128, 4, KC])
        ada_bT = vecs[:, 64:112]
        g_preT = vecs[:, 112:128]
        g_postT = vecs[:, 128:144]
        conv_wT = vecs[:, 144:192].rearrange("p (j k) -> p j k", j=3)
        conv_bT = vecs[:, 192:208]
        modT = sb("modT", [128, 48, 4])
        gmodT = sb("gmodT", [128, KC, 4])
        ggT = sb("ggT", [128, KC, 4])
        uhP_tk = sb("uhistP", [128, 2, KC])
        uhistP = uhP_tk[:].rearrange("p t k -> p k t")
        uhS_tk = [vecs[:, 256 + 32 * i: 256 + 32 * (i + 1)].rearrange("p (t k) -> p t k", t=2) for i in range(2)]
        uhistS = [u.rearrange("p t k -> p k t") for u in uhS_tk]
        ssb = sb("ssb", [128, 4])
        rmsb = sb("rmsb", [128, 4])
        rstd = sb("rstd", [128, 4])
        ss2 = sb("ss2", [128, 4])
        ss2p = sb("ss2p", [128, 4, 4])
        rms2 = sb("rms2", [128, 4])
        rstd2 = sb("rstd2", [128, 4])
        epsT = sb("epsT", [128, 1])
        identf = sb("identf", [128, 128])
        identb = sb("identb", [128, 128], BF16)
        onesb = sb("onesb", [128, 128], BF16)
        oneseg = sb("oneseg", [128, 3, 128])
        mask = {64: sb("mask64", [128, H, 64]), 32: sb("mask32", [128, H, 32])}
        qdec = {64: sb("qdec64", [128, H, 64]), 32: sb("qdec32", [128, H, 32])}
        kdec = {64: sb("kdec64", [128, H]), 32: sb("kdec32", [128, H])}
        ring = [ps("ring%d" % i, [128, 512]) for i in range(4)]
        oTp = [ps("oT%d" % i, [128, 512]) for i in range(2)]
        STps = ps("STps", [128, 4, 128])
        miscps = ps("miscps", [128, 512])
        ktrps = miscps[:, 0:256].bitcast(BF16).rearrange("p (b d) -> p b d", b=4)

        ring_i = [0]

        def ring_next():
            i = ring_i[0] % 4
            ring_i[0] += 1
            return ring[i], ("ring", i)

        def slot_regions(sid):
            if sid < 16:
                cb = sid
                return [(w_in[:, r * 2048 + cb * 128: r * 2048 + cb * 128 + 128], r * 128, 128) for r in range(4)]
            if sid < 32:
                j = sid - 16
                h, b = j // 2, j % 2
                if b == 0:
                    return [(w_in[:, 8192 + h * 128: 8192 + h * 128 + 128], 0, 128),
                            (w_in[:, 9216 + h * 128: 9216 + h * 128 + 128], 128, 128),
                            (w_in[:, 10240 + h * 256: 10240 + h * 256 + 256], 256, 256)]
                return [(w_in[:, 12288 + h * 256: 12288 + h * 256 + 256], 0, 256)]
            if sid < 48:
                m = sid - 32
                return [(w_in[:, 14336 + m * 128: 14336 + m * 128 + 128], 0, 128),
                        (w_in[:, 16384 + m * 128: 16384 + m * 128 + 128], 128, 128),
                        (w_br[0][:, m * 128: m * 128 + 128], 256, 128),
                        (w_br[1][:, m * 128: m * 128 + 128], 384, 128)]
            cs = sid - 48
            return [(w_out[:, cs * 512: cs * 512 + 512], 0, 512)]

        def slot_cols(sid):
            return sum(r[2] for r in slot_regions(sid))

        def small_load(dst_ap, src_ap, key, slow=False):
            P.op("sp", lambda e: e.dma_start(out=dst_ap, in_=src_ap, allow_slow_non_contiguous=slow),
                 writes=[key], dma="const")

        P.op("sp", lambda e: e.dma_start(out=stg[0][0:64, :], in_=c4.rearrange("s (k p) -> (s k) p", p=128)), writes=[("stg", 0)], dma="const")
        P.op("sp", lambda e: e.dma_start(out=stg[0][64:112, :], in_=ada_b.rearrange("(m p) -> m p", p=128)), joins=[("stg", 0)], dma="const")
        P.op("sp", lambda e: e.dma_start(out=stg[0][112:128, :], in_=norm_pre.rearrange("(k p) -> k p", p=128)), joins=[("stg", 0)], dma="const")
        P.op("sp", lambda e: e.dma_start(out=stg[1][0:16, :], in_=norm_post.rearrange("(k p) -> k p", p=128)), writes=[("stg", 1)], dma="const")
        P.op("sp", lambda e: e.dma_start(out=stg[1][16:64, :], in_=conv_w.rearrange("j (k p) -> (j k) p", p=128)), joins=[("stg", 1)], dma="const")
        P.op("sp", lambda e: e.dma_start(out=stg[1][64:80, :], in_=conv_b.rearrange("(k p) -> k p", p=128)), joins=[("stg", 1)], dma="const")
        P.op("sp", lambda e: e.dma_start(out=stg[2][0:64, :], in_=sconv.rearrange("i t (k p) -> (i t k) p", p=128)), writes=[("stg", 2)], dma="const")
        small_load(identf[:], cst["identf"], "identf")
        small_load(oneseg[:], cst["oneseg"], "oneseg")
        for L in (64, 32):
            small_load(mask[L][:], cst["mask%d" % L], ("mask", L))
            small_load(qdec[L][:], cst["qdec%d" % L], ("qdec", L))
            small_load(kdec[L][:], cst["kdec%d" % L], ("kdec", L))
        P.op("pool", lambda e: e.memset(epsT[:], EPS), writes=["epsT"])
        P.op("pool", lambda e: e.memset(onesb[:], 1.0), writes=["onesb"])
        P.op("dve", lambda e: e.tensor_copy(out=identb[:], in_=identf[:]), reads=["identf"], writes=["identb"])

        slot_seq = []
        TILE_SLOTS = list(range(16, 32)) + list(range(0, 16)) + list(range(32, NSLOT))
        loaded = [0]
        use_i = [0]

        seen = {}

        def ensure_loaded(upto):
            while loaded[0] <= upto and loaded[0] < len(slot_seq):
                j = loaded[0]
                sid = slot_seq[j]
                n = slot_cols(sid)
                ri = j % 3
                t = wr[ri]
                st_ = seen.get(sid, 0)
                if st_ < 2:
                    for r_i, (src, c0, nn) in enumerate(slot_regions(sid)):
                        kw = dict(writes=[("wr", ri)]) if r_i == 0 else dict(joins=[("wr", ri)])
                        P.op("pool", lambda e, t=t, src=src, c0=c0, nn=nn: e.dma_start(
                            out=t[:, :, c0:c0 + nn], in_=src.rearrange("(k p) n -> p k n", p=128)), dma=("wrc", ri), **kw)
                    if st_ == 1 or sid % 3 == 0:
                        P.op("sp", lambda e, t=t, sid=sid, n=n: e.dma_start(out=wsc[sid][:, :, 0:n], in_=t[:, :, 0:n]),
                             reads=[("wr", ri)], writes=[("wsc", sid)], dma=("wb", ri))
                        seen[sid] = 2
                    else:
                        seen[sid] = 1
                else:
                    P.op("sp", lambda e, t=t, sid=sid, n=n: e.dma_start(out=t[:, :, 0:n], in_=wsc[sid][:, :, 0:n]),
                         reads=[("wsc", sid)], writes=[("wr", ri)], dma=("wr", ri))
                loaded[0] += 1

        def use_slot():
            j = use_i[0]
            use_i[0] += 1
            ensure_loaded(j + 2)
            return wr[j % 3], ("wr", j % 3)

        nrows = [128, 80, 64]
        for i in range(3):
            R, rk = ring_next()
            n_ = nrows[i]
            P.op("pe", lambda e, R=R, i=i, n_=n_: e.transpose(R[:, 0:n_], stg[i][0:n_, :], identf[0:n_, 0:n_]),
                 reads=[("stg", i), "identf"], writes=[rk])
            kw = dict(writes=["vecs"]) if i == 0 else dict(joins=["vecs"])
            if i == 2:
                kw = dict(joins=["vecs"], writes=[("uhistS", 0), ("uhistS", 1)])
            P.op("act", lambda e, R=R, i=i, n_=n_: e.activation(out=vecs[:, 128 * i: 128 * i + n_], in_=R[:, 0:n_], func=AF.Copy),
                 reads=[rk], **kw)
        P.op("act", lambda e: e.activation(out=scT[:], in_=cT[:], func=AF.Silu), reads=["vecs"], writes=["scT"])
        modps, modkey = ring_next()
        scTb = sb("scTb", [128, 4, KC], BF16)
        P.op("dve", lambda e: e.tensor_copy(out=scTb[:], in_=scT[:]), reads=["scT"], writes=["scTb"])
        for mp in range(12):
            aw = wr[mp % 3]
            P.op("pool", lambda e, aw=aw, mp=mp: e.dma_start(out=aw[:], in_=ada_w[:, mp * 512:(mp + 1) * 512].rearrange("(k p) n -> p k n", p=128)),
                 writes=[("wr", mp % 3)], dma=("wrc", mp % 3))
            for sub in range(4):
                mt = 4 * mp + sub
                for kc in range(KC):
                    P.op("pe", lambda e, aw=aw, mt=mt, kc=kc, sub=sub: e.matmul(modps[:, mt * 4:(mt + 1) * 4], lhsT=aw[:, kc, sub * 128:(sub + 1) * 128],
                                                                            rhs=scTb[:, :, kc], start=(kc == 0), stop=(kc == KC - 1)),
                         reads=[("wr", mp % 3), "scTb"], writes=[modkey])
        P.op("dve", lambda e: e.tensor_tensor(out=modT[:], in0=modps[:, 0:192].rearrange("p (m s) -> p m s", s=4),
                                              in1=ada_bT[:, :, None].broadcast_to([128, 48, 4]), op=ALU.add),
             reads=[modkey, "vecs"], writes=["modT"])
        P.op("dve", lambda e: e.scalar_tensor_tensor(out=gmodT[:], in0=modT[:, 16:32, :], scalar=1.0,
                                                     in1=g_preT[:, :, None].broadcast_to([128, KC, 4]), op0=ALU.add, op1=ALU.mult),
             reads=["modT", "vecs"], writes=["gmodT"])
        P.op("dve", lambda e: e.scalar_tensor_tensor(out=ggT[:], in0=modT[:, 32:48, :], scalar=0.5,
                                                     in1=g_postT[:, :, None].broadcast_to([128, KC, 4]), op0=ALU.mult, op1=ALU.mult),
             reads=["modT", "vecs"], writes=["ggT"])

        tiles = []
        for s in range(nseq if npt > 0 else 0):
            for t0 in range(0, SEQ, T):
                tiles.append(dict(kind="p", ntok=T, L=64, xsrc=xp, ydst=yp, row0=s * SEQ + t0, pos0=t0, seq=s,
                                  first=(t0 == 0), last=(t0 == SEQ - T),
                                  segs=[dict(col0=0, n=T, mod=s)], blocks=[(b * 128, 128) for b in range(4)]))
        if with_sample:
          tiles.append(dict(kind="s", ntok=2 * DEC, L=32, xsrc=xs, ydst=ys, row0=0, pos0=0, seq=0, first=True, last=True,
                          segs=[dict(col0=0, n=DEC, mod=2), dict(col0=DEC, n=DEC, mod=3)], blocks=[(0, 64)]))
        for _ in tiles:
            slot_seq.extend(TILE_SLOTS)

        def HK(kc):
            return ("hT", kc)

        class TileEmit:
            pass

        def make_tile(tl):
            aux = "dve" if tl.get("idx", 2) <= 1 else "pool"
            ntok = tl["ntok"]
            L = tl["L"]
            blocks = tl["blocks"]
            segs = tl["segs"]
            xsrc, ydst, row0 = tl["xsrc"], tl["ydst"], tl["row0"]
            nblk = len(blocks)
            isS = tl["kind"] == "s"
            if isS:
                chunks = [(0, 32, 0, 0, 0), (32, 32, 0, 32, 1)]
            else:
                chunks = [(c * 64, 64, c // 2, (c % 2) * 64, 0) for c in range(8)]
            junkb = onorm[:].rearrange("p a t -> p (a t)").bitcast(BF16)
            mgf = merged[:].rearrange("p k t -> p (k t)").bitcast(F32)
            ggh = [mgf[:, 1024 * i: 1024 * (i + 1)] for i in range(2)]
            TE = TileEmit()

            def rope_tabs():
                if isS:
                    P.op("sp", lambda e: e.dma_start(out=cosT[:, 0:64], in_=cst["cosS"]), writes=["cosT"], dma="ropec")
                    P.op("sp", lambda e: e.dma_start(out=sinX[:, 0:64], in_=cst["sinS"]), writes=["sinX"], dma="ropes")
                else:
                    p0 = tl["pos0"]
                    P.op("sp", lambda e, p0=p0: e.dma_start(out=cosT[:], in_=cst["cosP"][:, p0:p0 + T]), writes=["cosT"], dma="ropec")
                    P.op("sp", lambda e, p0=p0: e.dma_start(out=sinX[:], in_=cst["sinP"][:, p0:p0 + T]), writes=["sinX"], dma="ropes")

            def stats():
                P.op("pool", lambda e: e.memset(ssb[:], 0.0), writes=["ssb"])
                for bi, (b0, nb) in enumerate(blocks):
                    x_ = xt[bi % 2]
                    P.op("act", lambda e, x_=x_, b0=b0, nb=nb: e.dma_start(out=x_[0:nb, :], in_=xsrc[row0 + b0: row0 + b0 + nb, :]),
                         writes=[("xt", bi % 2)], dma=("xt", bi % 2))
                    P.op("act", lambda e, x_=x_, nb=nb, bi=bi: e.activation(out=junkb[0:nb, :], in_=x_[0:nb, :], func=AF.Square,
                                                                            accum_out=ssb[0:nb, bi:bi + 1]),
                         reads=[("xt", bi % 2), "ssb"], writes=["ssb", "onorm"])
                P.op("act", lambda e: e.activation(out=rmsb[:, 0:nblk], in_=ssb[:, 0:nblk], func=AF.Sqrt, scale=1.0 / D, bias=epsT[:]),
                     reads=["ssb", "epsT"], writes=["rmsb"])
                P.op("dve", lambda e: e.reciprocal(out=rstd[:, 0:nblk], in_=rmsb[:, 0:nblk]), reads=["rmsb"], writes=["rstd"])

            szv = [szr[i][:].rearrange("p a t -> p (a t)") for i in range(2)]

            def xparts(bi):
                if bi == 2:
                    return [(szv[0], ("szr", 0), 0, ("x2", 0)), (szv[1], ("szr", 1), 1024, ("x2", 1))]
                return [(xt[bi % 2], ("xt", bi % 2), 0, ("xt", bi % 2))]

            def x_dma_scale(bi):
                b0, nb = blocks[bi]
                for (xa, xk, c0, dk) in xparts(bi):
                    nc_ = xa.shape[-1]
                    P.op("act", lambda e, xa=xa, b0=b0, nb=nb, c0=c0, nc_=nc_: e.dma_start(out=xa[0:nb, :], in_=xsrc[row0 + b0: row0 + b0 + nb, c0:c0 + nc_]),
                         writes=[xk], dma=dk)
                    P.op("dve", lambda e, xa=xa, nb=nb, bi=bi: e.tensor_scalar(out=xa[0:nb, :], in0=xa[0:nb, :], scalar1=rstd[0:nb, bi:bi + 1],
                                                                               scalar2=None, op0=ALU.mult),
                         reads=[xk, "rstd"], writes=[xk])

            def xload_pre():
                for bi in range(min(3, nblk)):
                    x_dma_scale(bi)

            hw_first = [True] * KC

            def xload(bis=None):
                for bi, (b0, nb) in enumerate(blocks):
                    if bis is not None and bi not in bis:
                        continue
                    parts = xparts(bi)
                    for g in range(4):
                        R, rk = ring_next()
                        for j in range(4):
                            kc = 4 * g + j
                            xa, xk, c0, _dk = parts[(kc * 128) // 1024] if len(parts) == 2 else parts[0]
                            lo_c = kc * 128 - c0
                            P.op("pe", lambda e, R=R, xa=xa, nb=nb, j=j, lo_c=lo_c: e.transpose(R[:, j * 128: j * 128 + nb], xa[0:nb, lo_c:lo_c + 128],
                                                                                              identf[0:nb, 0:nb]),
                                 reads=[xk, "identf"], writes=[rk])
                        for j in range(4):
                            kc = 4 * g + j
                            for sg in segs:
                                lo = max(sg["col0"], b0)
                                hi = min(sg["col0"] + sg["n"], b0 + nb)
                                if hi <= lo:
                                    continue
                                md = sg["mod"]
                                kw = dict(writes=[HK(kc)]) if hw_first[kc] else dict(joins=[HK(kc)])
                                hw_first[kc] = False
                                src = R[:, j * 128 + (lo - b0): j * 128 + (hi - b0)]
                                dst = hT[:, kc, lo:hi]
                                if g % 2 == 0:
                                    P.op("act", lambda e, src=src, dst=dst, kc=kc, md=md: e.activation(
                                        out=dst, in_=src, func=AF.Identity, scale=gmodT[:, kc, md:md + 1], bias=modT[:, kc, md:md + 1]),
                                        reads=[rk, "gmodT", "modT"], **kw)
                                else:
                                    P.op("dve", lambda e, src=src, dst=dst, kc=kc, md=md: e.tensor_scalar(
                                        out=dst, in0=src, scalar1=gmodT[:, kc, md:md + 1], scalar2=modT[:, kc, md:md + 1],
                                        op0=ALU.mult, op1=ALU.add),
                                        reads=[rk, "gmodT", "modT"], **kw)
                    if bi == 1 and nblk > 3:
                        x_dma_scale(3)

            def unit_fm_g(wt, wk, c0, src3, srckey_fn):
                R, rk = ring_next()
                for kc in range(KC):
                    P.op("pe", lambda e, R=R, wt=wt, c0=c0, kc=kc, src3=src3: e.matmul(
                        R[:, 0:ntok], lhsT=wt[:, kc, c0:c0 + 128], rhs=src3[:, kc, 0:ntok], start=(kc == 0), stop=(kc == KC - 1)),
                        reads=[wk, srckey_fn(kc)], writes=[rk])
                    if kc == KC // 2 - 1:
                        yield
                return R, rk

            def unit_fm(wt, wk, c0, src3, srckey_fn):
                g = unit_fm_g(wt, wk, c0, src3, srckey_fn)
                try:
                    while True:
                        next(g)
                except StopIteration as ex:
                    return ex.value

            def conv_gen(cb):
                if cb == 0 and tl["kind"] == "p" and tl["first"]:
                    P.op("pool", lambda e: e.memset(uhP_tk[:], 0.0), writes=["uhistP"])
                wt, wk = use_slot()
                Rgc, kgc = yield from unit_fm_g(wt, wk, 128, hT, HK)
                P.op("act", lambda e, Rgc=Rgc: e.activation(out=gcs[:, 0:ntok], in_=Rgc[:, 0:ntok], func=AF.Copy),
                     reads=[kgc], writes=["gcs"])
                yield
                Rgu, kgu = yield from unit_fm_g(wt, wk, 256, hT, HK)
                for si, sg in enumerate(segs):
                    base = sg["col0"] + 2 * si
                    n = sg["n"]
                    c0 = sg["col0"]
                    uh, uhk = (uhistS[si], ("uhistS", si)) if isS else (uhistP, "uhistP")
                    P.op(aux, lambda e, uh=uh, base=base, cb=cb: e.tensor_copy(out=ubuf[:, base:base + 2], in_=uh[:, cb, :]),
                         reads=[uhk], writes=[("ubh", si)])
                    P.op("dve", lambda e, Rgu=Rgu, base=base, n=n, c0=c0: e.tensor_tensor(
                        out=ubuf[:, base + 2: base + 2 + n], in0=Rgu[:, c0:c0 + n], in1=gcs[:, c0:c0 + n], op=ALU.mult),
                        reads=[kgu, "gcs"], writes=[("ubd", si)])
                    P.op(aux, lambda e, uh=uh, base=base, n=n, cb=cb: e.tensor_copy(out=uh[:, cb, :], in_=ubuf[:, base + n: base + n + 2]),
                         reads=[("ubd", si)], writes=[uhk])
                    P.op("act", lambda e, base=base, n=n, c0=c0, cb=cb: e.activation(
                        out=cbuf[:, c0:c0 + n], in_=ubuf[:, base + 2: base + 2 + n], func=AF.Identity,
                        scale=conv_wT[:, 2, cb:cb + 1], bias=conv_bT[:, cb:cb + 1]),
                        reads=[("ubd", si), "vecs"], writes=[("cbuf", si)])
                    P.op("dve", lambda e, base=base, n=n, c0=c0, cb=cb: e.scalar_tensor_tensor(
                        out=cbuf[:, c0:c0 + n], in0=ubuf[:, base + 1: base + 1 + n], scalar=conv_wT[:, 1, cb:cb + 1],
                        in1=cbuf[:, c0:c0 + n], op0=ALU.mult, op1=ALU.add),
                        reads=[("ubd", si), ("ubh", si), "vecs", ("cbuf", si)], writes=[("cbuf", si)])
                    P.op("dve", lambda e, base=base, n=n, c0=c0, cb=cb: e.scalar_tensor_tensor(
                        out=cbuf[:, c0:c0 + n], in0=ubuf[:, base: base + n], scalar=conv_wT[:, 0, cb:cb + 1],
                        in1=cbuf[:, c0:c0 + n], op0=ALU.mult, op1=ALU.add),
                        reads=[("ubd", si), ("ubh", si), "vecs", ("cbuf", si)], writes=[("cbuf", si)])
                yield
                Rzc, kzc = yield from unit_fm_g(wt, wk, 384, hT, HK)
                P.op("act", lambda e, Rzc=Rzc: e.activation(out=szc[:, 0:ntok], in_=Rzc[:, 0:ntok], func=AF.Silu),
                     reads=[kzc], writes=["szc"])
                yield
                Rgb, kgb = yield from unit_fm_g(wt, wk, 0, hT, HK)
                P.op("dve", lambda e, Rgb=Rgb: e.tensor_tensor(out=tcv[:, 0:ntok], in0=Rgb[:, 0:ntok], in1=cbuf[:, 0:ntok], op=ALU.mult),
                     reads=[kgb] + [("cbuf", si) for si in range(len(segs))], writes=["tcv"])
                P.op(aux, lambda e, cb=cb: e.tensor_tensor(out=yconv[:, cb, 0:ntok], in0=tcv[:, 0:ntok], in1=szc[:, 0:ntok], op=ALU.mult),
                     reads=["tcv", "szc"], writes=[("Y", cb)])
                yield

            def conv_out(src_tk, skey, dst_seq):
                for t_ in range(2):
                    R, rk = ring_next()
                    P.op("pe", lambda e, R=R, t_=t_: e.transpose(R[0:KC, 0:128], src_tk[:, t_, :], identf[:, :]),
                         reads=[skey, "identf"], writes=[rk])
                    P.op("act", lambda e, R=R: e.activation(out=ostg[:, :], in_=R[0:KC, 0:128], func=AF.Copy), reads=[rk], writes=["ostg"])
                    P.op("sp", lambda e, t_=t_: e.dma_start(out=dst_seq[t_].rearrange("(k p) -> k p", p=128), in_=ostg[:, :]),
                         reads=["ostg"], dma="outs")

            def conv_state_out():
                if isS:
                    for si in range(2):
                        conv_out(uhS_tk[si], ("uhistS", si), ncs[si])
                elif tl["last"]:
                    conv_out(uhP_tk[:], "uhistP", ncp[tl["seq"]])

            def Skeys(slot, h):
                if slot == 0:
                    return ("S", h), ("Sbf", h)
                return ("xt", 0), ("xt", 1)

            Ssl = [S0[:], S1]
            Sbsl = [Sbf0[:], Sbf1]

            def state_init():
                if isS:
                    for sl in range(2):
                        P.op("sp", lambda e, sl=sl: e.dma_start(out=Ssl[sl], in_=sret[sl].rearrange("h d e -> d h e")),
                             writes=[Skeys(sl, h)[0] for h in range(H)] if sl == 0 else [("xt", 0)], dma=("sin", sl))
                        P.op("act", lambda e, sl=sl: e.activation(out=Sbsl[sl], in_=Ssl[sl], func=AF.Copy),
                             reads=[Skeys(sl, h)[0] for h in range(H)] if sl == 0 else [("xt", 0)],
                             writes=[Skeys(sl, h)[1] for h in range(H)] if sl == 0 else [("xt", 1)])
                elif tl["first"]:
                    P.op("pool", lambda e: e.memset(S0[:], 0.0), writes=[("S", h) for h in range(H)])
                    P.op("pool", lambda e: e.memset(Sbf0[:], 0.0), writes=[("Sbf", h) for h in range(H)])

            def rope_to(dst, dkey):
                P.op("dve", lambda e: e.tensor_tensor(out=ropeA[:, 0:ntok], in0=rs[:, 0:ntok], in1=cosT[:, 0:ntok], op=ALU.mult),
                     reads=["rs", "cosT"], writes=["ropeA"])
                P.op("dve", lambda e: e.tensor_tensor(out=ropeB[0:64, 0:ntok], in0=rs[64:128, 0:ntok], in1=sinX[64:128, 0:ntok], op=ALU.mult),
                     reads=["rs", "sinX"], writes=["ropeB0"])
                P.op("dve", lambda e: e.tensor_tensor(out=ropeB[64:128, 0:ntok], in0=rs[0:64, 0:ntok], in1=sinX[0:64, 0:ntok], op=ALU.mult),
                     reads=["rs", "sinX"], writes=["ropeB1"])
                P.op("dve", lambda e: e.tensor_tensor(out=dst[:, 0:ntok], in0=ropeA[:, 0:ntok], in1=ropeB[:, 0:ntok], op=ALU.add),
                     reads=["ropeA", "ropeB0", "ropeB1"], writes=[dkey])

            def proj_gen(h):
                pb = h % 2
                wt, wk = use_slot()
                nch = ntok // L
                Rq, kq = yield from unit_fm_g(wt, wk, 0, hT, HK)
                P.op("dve", lambda e, Rq=Rq, h=h: e.tensor_tensor(
                    out=rs[:, 0:ntok].rearrange("p (c l) -> p c l", l=L), in0=Rq[:, 0:ntok].rearrange("p (c l) -> p c l", l=L),
                    in1=qdec[L][:, h, None, :].broadcast_to([128, nch, L]), op=ALU.mult),
                    reads=[kq, ("qdec", L)], writes=["rs"])
                rope_to(qT[pb], ("qT", pb))
                yield
                Rk, kk = yield from unit_fm_g(wt, wk, 128, hT, HK)
                P.op("act", lambda e, Rk=Rk: e.activation(out=rs[:, 0:ntok], in_=Rk[:, 0:ntok], func=AF.Copy), reads=[kk], writes=["rs"])
                rope_to(kT[pb], ("kT", pb))
                yield
                for g0 in range(0, nblk, 2):
                    R, rk = ring_next()
                    gb = blocks[g0:g0 + 2]
                    for j, (b0, nb) in enumerate(gb):
                        for kc in range(KC):
                            P.op("pe", lambda e, R=R, j=j, b0=b0, nb=nb, kc=kc, wt=wt: e.matmul(
                                R[0:nb, j * 256:(j + 1) * 256], lhsT=hT[:, kc, b0:b0 + nb], rhs=wt[:, kc, 256:512],
                                start=(kc == 0), stop=(kc == KC - 1)),
                                reads=[wk, HK(kc)], writes=[rk])
                        if j == 0 and len(gb) > 1:
                            yield
                    nbm = gb[0][1]
                    ng = len(gb)
                    kw = dict(writes=[("vtok", pb)]) if g0 == 0 else dict(joins=[("vtok", pb)])
                    P.op("act", lambda e, R=R, g0=g0, ng=ng, nbm=nbm, pb=pb: e.activation(
                        out=vtok[pb][0:nbm, g0:g0 + ng, :], in_=R[0:nbm, 0:ng * 256].rearrange("p (b e) -> p b e", e=256), func=AF.Copy),
                        reads=[rk], **kw)
                    yield
                for bi, (b0, nb) in enumerate(blocks):
                    P.op("pe", lambda e, bi=bi, b0=b0, nb=nb, pb=pb: e.transpose(ktrps[0:nb, bi, :], kT[pb][:, b0:b0 + nb], identb[:, :]),
                         reads=[("kT", pb), "identb"], writes=["ktrps"])
                nbm = blocks[0][1]
                P.op("dve", lambda e, h=h, pb=pb, nbm=nbm: e.tensor_scalar(
                    out=ktok[pb][0:nbm, 0:nblk, :], in0=ktrps[0:nbm, 0:nblk, :], scalar1=kdec[L][0:nbm, h:h + 1], scalar2=None, op0=ALU.mult),
                    reads=["ktrps", ("kdec", L)], writes=[("ktok", pb)])
                wt2, wk2 = use_slot()
                for m in range(2):
                    R, rk = yield from unit_fm_g(wt2, wk2, m * 128, hT, HK)
                    kw = dict(writes=[("szr", pb)]) if m == 0 else dict(joins=[("szr", pb)])
                    P.op("act", lambda e, R=R, m=m, pb=pb: e.activation(out=szr[pb][:, m, 0:ntok], in_=R[:, 0:ntok], func=AF.Silu),
                         reads=[rk], **kw)
                    yield
                for bi, (b0, nb) in enumerate(blocks):
                    P.op("pe", lambda e, bi=bi, b0=b0, nb=nb, pb=pb: e.matmul(
                        STps[0:nb, bi, 0:nb], lhsT=kT[pb][:, b0:b0 + nb], rhs=qT[pb][:, b0:b0 + nb], start=True, stop=True),
                        reads=[("kT", pb), ("qT", pb)], writes=["STps"])
                nper = 2
                for po_i in range(nper):
                    po = po_i * L
                    kw = dict(writes=["STm"]) if po_i == 0 else dict(joins=["STm"])
                    P.op("dve", lambda e, po=po, h=h: e.tensor_tensor(
                        out=STm[po:po + L, 0:nblk, 0:L], in0=STps[po:po + L, 0:nblk, po:po + L],
                        in1=mask[L][po:po + L, h, None, :].broadcast_to([L, nblk, L]), op=ALU.mult),
                        reads=["STps", ("mask", L)], **kw)
                yield

            def ret_gen(h):
                pb = h % 2
                gLh = float(gpow[1 if isS else 0][h])
                for (c0, Lc, bi, po, sl) in chunks:
                    sk, sbk = Skeys(sl, h)
                    S_, Sb_ = Ssl[sl], Sbsl[sl]
                    for ec in range(2):
                        P.op("pe", lambda e, ec=ec, c0=c0, Lc=Lc, bi=bi, po=po, pb=pb: e.matmul(
                            oTp[ec][:, c0:c0 + Lc], lhsT=vtok[pb][po:po + Lc, bi, ec * 128:(ec + 1) * 128], rhs=STm[po:po + Lc, bi, 0:Lc],
                            start=True, stop=False),
                            reads=[("vtok", pb), "STm"], writes=[("oT", ec)])
                        P.op("pe", lambda e, ec=ec, c0=c0, Lc=Lc, Sb_=Sb_, h=h, pb=pb: e.matmul(
                            oTp[ec][:, c0:c0 + Lc], lhsT=Sb_[:, h, ec * 128:(ec + 1) * 128], rhs=qT[pb][:, c0:c0 + Lc],
                            start=False, stop=True),
                            reads=[sbk, ("qT", pb)], writes=[("oT", ec)])
                    Rd, rdk = ring_next()
                    P.op("pe", lambda e, Rd=Rd, Lc=Lc, bi=bi, po=po, pb=pb: e.matmul(
                        Rd[:, 0:DV], lhsT=ktok[pb][po:po + Lc, bi, :], rhs=vtok[pb][po:po + Lc, bi, :], start=True, stop=True),
                        reads=[("ktok", pb), ("vtok", pb)], writes=[rdk])
                    P.op("dve", lambda e, Rd=Rd, S_=S_, Sb_=Sb_, h=h, gLh=gLh: e.scalar_tensor_tensor(
                        out=Sb_[:, h, :], in0=S_[:, h, :], scalar=gLh, in1=Rd[:, 0:DV], op0=ALU.mult, op1=ALU.add),
                        reads=[sk, rdk], writes=[sbk])
                    P.op("dve", lambda e, Rd=Rd, S_=S_, h=h, gLh=gLh: e.scalar_tensor_tensor(
                        out=S_[:, h, :], in0=S_[:, h, :], scalar=gLh, in1=Rd[:, 0:DV], op0=ALU.mult, op1=ALU.add),
                        reads=[sk, rdk], writes=[sk])
                    yield
                for ec in range(2):
                    kw = dict(writes=["osq"]) if ec == 0 else dict(joins=["osq"])
                    P.op("act", lambda e, ec=ec: e.activation(out=osq[:, ec, 0:ntok], in_=oTp[ec][:, 0:ntok], func=AF.Square),
                         reads=[("oT", ec)], **kw)
                yield
                R, rk = ring_next()
                for ec in range(2):
                    P.op("pe", lambda e, R=R, ec=ec: e.matmul(R[:, 0:ntok], lhsT=onesb[:, :], rhs=osq[:, ec, 0:ntok], start=(ec == 0), stop=(ec == 1)),
                         reads=["osq", "onesb"], writes=[rk])
                P.op("act", lambda e, R=R: e.activation(out=rms[:, 0:ntok], in_=R[:, 0:ntok], func=AF.Ln, scale=1.0 / DV, bias=epsT[:]),
                     reads=[rk, "epsT"], writes=["rms"])
                P.op("act", lambda e: e.activation(out=rinv[:, 0:ntok], in_=rms[:, 0:ntok], func=AF.Exp, scale=-0.5), reads=["rms"], writes=["rinv"])
                for ec in range(2):
                    kw = dict(writes=["onorm"]) if ec == 0 else dict(joins=["onorm"])
                    P.op("dve", lambda e, ec=ec: e.tensor_tensor(out=onorm[:, ec, 0:ntok], in0=oTp[ec][:, 0:ntok], in1=rinv[:, 0:ntok], op=ALU.mult),
                         reads=[("oT", ec), "rinv"], **kw)
                P.op(aux, lambda e, h=h, pb=pb: e.tensor_tensor(out=yret[:, 2 * h:2 * h + 2, 0:ntok], in0=onorm[:, :, 0:ntok],
                                                                    in1=szr[pb][:, :, 0:ntok], op=ALU.mult),
                     reads=["onorm", ("szr", pb)], writes=[("Y", 16 + 2 * h), ("Y", 16 + 2 * h + 1)])
                yield

            def interleave(main, side):
                dm = ds = False
                while not (dm and ds):
                    if not dm:
                        try:
                            next(main)
                        except StopIteration:
                            dm = True
                    if not ds:
                        try:
                            next(side)
                        except StopIteration:
                            ds = True

            def chain(*gens):
                for g in gens:
                    for _ in g:
                        yield

            def empty():
                return
                yield

            def body(prev_post=None):
                if prev_post is None:
                    prev_post = empty()
                if isS:
                    for _ in prev_post:
                        pass
                    prev_post = empty()
                state_init()
                for h in range(H):
                    interleave(proj_gen(h), ret_gen(h - 1) if h >= 1 else prev_post)
                interleave(chain(*[conv_gen(cb) for cb in range(KC)]), ret_gen(H - 1))
                conv_state_out()
                if isS:
                    for sl in range(2):
                        P.op("sp", lambda e, sl=sl: e.dma_start(out=nrs[sl].rearrange("h d e -> d h e"), in_=Ssl[sl]),
                             reads=[Skeys(sl, h)[0] for h in range(H)] if sl == 0 else [("xt", 0)], dma="outs")
                elif tl["last"]:
                    sq = tl["seq"]
                    P.op("sp", lambda e, sq=sq: e.dma_start(out=nrp[sq].rearrange("h d e -> d h e"), in_=S0[:]),
                         reads=[("S", h) for h in range(H)], dma="outs")

            def merge():
                for m in range(KC):
                    wt, wk = use_slot()
                    R1, k1 = unit_fm(wt, wk, 0, hT, HK)
                    P.op("act", lambda e, R1=R1: e.activation(out=ta[:, 0:ntok], in_=R1[:, 0:ntok], func=AF.Tanh, scale=0.5), reads=[k1], writes=["gcs"])
                    R2, k2 = unit_fm(wt, wk, 256, yconv, lambda kc: ("Y", kc))
                    P.op("dve", lambda e, R2=R2: e.scalar_tensor_tensor(out=mA[:, 0:ntok], in0=ta[:, 0:ntok], scalar=1.0, in1=R2[:, 0:ntok],
                                                                        op0=ALU.add, op1=ALU.mult), reads=["gcs", k2], writes=["tcv"])
                    R3, k3 = unit_fm(wt, wk, 128, hT, HK)
                    P.op("act", lambda e, R3=R3: e.activation(out=tb_[:, 0:ntok], in_=R3[:, 0:ntok], func=AF.Tanh, scale=0.5), reads=[k3], writes=["szc"])
                    R4, k4 = unit_fm(wt, wk, 384, yret, lambda kc: ("Y", 16 + kc))
                    P.op("dve", lambda e, R4=R4: e.scalar_tensor_tensor(out=mB[:, 0:ntok], in0=tb_[:, 0:ntok], scalar=1.0, in1=R4[:, 0:ntok],
                                                                        op0=ALU.add, op1=ALU.mult), reads=["szc", k4], writes=["rs"])
                    P.op(aux, lambda e, m=m: e.tensor_tensor(out=merged[:, m, 0:ntok], in0=mA[:, 0:ntok], in1=mB[:, 0:ntok], op=ALU.add),
                         reads=["tcv", "rs"], writes=[("mg", m)])

            def outp(cs_list):
                if 0 in cs_list:
                    P.op("pool", lambda e: e.memset(ss2p[:], 0.0), writes=["ss2p"])
                for cs in cs_list:
                    wt, wk = use_slot()
                    for bi, (b0, nb) in enumerate(blocks):
                        R, rk = ring_next()
                        for kc in range(KC):
                            P.op("pe", lambda e, R=R, b0=b0, nb=nb, kc=kc, wt=wt: e.matmul(
                                R[0:nb, :], lhsT=merged[:, kc, b0:b0 + nb], rhs=wt[:, kc, 0:512], start=(kc == 0), stop=(kc == KC - 1)),
                                reads=[wk, ("mg", kc)], writes=[rk])
                        P.op("act", lambda e, R=R, nb=nb, bi=bi, cs=cs: e.activation(out=outsb[0:nb, bi, cs * 512:(cs + 1) * 512], in_=R[0:nb, :], func=AF.Copy),
                             reads=[rk], writes=[("Y", 8 * bi + 2 * cs), ("Y", 8 * bi + 2 * cs + 1)])
                        P.op("act", lambda e, R=R, nb=nb, bi=bi, cs=cs: e.activation(out=junkb[0:nb, 0:512], in_=R[0:nb, :], func=AF.Square,
                                                                                 accum_out=ss2p[0:nb, bi, cs:cs + 1]),
                             reads=[rk, "ss2p"], writes=["ss2p", "onorm"])

            def ggbuild_gen():
                nbk = blocks[0][1]
                di = 0
                for g in range(4):
                    R, rk = ring_next()
                    for j in range(4):
                        kc = 4 * g + j
                        for si, sg in enumerate(segs):
                            md = sg["mod"]
                            dg = diag[di % 2]
                            dkey = ("diag", di % 2)
                            di += 1
                            P.op("dve", lambda e, dg=dg, kc=kc, md=md: e.tensor_scalar(out=dg[:], in0=identf[:], scalar1=ggT[:, kc, md:md + 1],
                                                                                      scalar2=None, op0=ALU.mult),
                                 reads=["identf", "ggT"], writes=[dkey])
                            sidx = 0 if not isS else 1 + si
                            P.op("pe", lambda e, R=R, j=j, dg=dg, sidx=sidx, si=si: e.matmul(
                                R[0:nbk, j * 128:(j + 1) * 128], lhsT=oneseg[:, sidx, 0:nbk], rhs=dg[:], start=(si == 0), stop=(si == len(segs) - 1)),
                                reads=[dkey, "oneseg"], writes=[rk])
                    gh = ggh[g // 2]
                    P.op("act", lambda e, R=R, g=g, gh=gh: e.activation(out=gh[0:nbk, (g % 2) * 512:(g % 2 + 1) * 512], in_=R[0:nbk, :], func=AF.Copy),
                         reads=[rk], writes=[("mg", 2 * g), ("mg", 2 * g + 1)])
                    yield

            def post_stats():
                P.op("dve", lambda e: e.tensor_reduce(out=ss2[:, 0:nblk], in_=ss2p[:, 0:nblk, :], axis=mybir.AxisListType.X, op=ALU.add),
                     reads=["ss2p"], writes=["ss2"])
                P.op("act", lambda e: e.activation(out=rms2[:, 0:nblk], in_=ss2[:, 0:nblk], func=AF.Sqrt, scale=1.0 / (4.0 * D), bias=epsT[:]),
                     reads=["ss2", "epsT"], writes=["rms2"])
                P.op("dve", lambda e: e.reciprocal(out=rstd2[:, 0:nblk], in_=rms2[:, 0:nblk]), reads=["rms2"], writes=["rstd2"])

            def post_gen():
                yield from ggbuild_gen()
                for bi, (b0, nb) in enumerate(blocks):
                    x_ = xt[bi % 2]
                    yk = [("Y", 8 * bi + j) for j in range(8)]
                    P.op("act", lambda e, x_=x_, b0=b0, nb=nb: e.dma_start(out=x_[0:nb, :], in_=xsrc[row0 + b0: row0 + b0 + nb, :]),
                         writes=[("xt", bi % 2)], dma=("xt", bi % 2))
                    for hf in range(2):
                        ykh = yk[4 * hf: 4 * hf + 4]
                        P.op("dve", lambda e, nb=nb, bi=bi, hf=hf: e.scalar_tensor_tensor(
                            out=outsb[0:nb, bi, hf * 1024:(hf + 1) * 1024], in0=outsb[0:nb, bi, hf * 1024:(hf + 1) * 1024],
                            scalar=rstd2[0:nb, bi:bi + 1], in1=ggh[hf][0:nb, :], op0=ALU.mult, op1=ALU.mult),
                            reads=ykh + ["rstd2"] + [("mg", 4 * hf + q_) for q_ in range(4)], writes=ykh)
                    P.op("pool", lambda e, nb=nb, bi=bi, x_=x_: e.tensor_tensor(out=outsb[0:nb, bi, :], in0=outsb[0:nb, bi, :], in1=x_[0:nb, :], op=ALU.add),
                         reads=yk + [("xt", bi % 2)], writes=yk)
                    P.op("pool", lambda e, nb=nb, bi=bi, b0=b0: e.dma_start(out=ydst[row0 + b0: row0 + b0 + nb, :], in_=outsb[0:nb, bi, :]),
                         reads=yk, dma="outy")
                    yield

            TE.rope_tabs, TE.stats, TE.xload, TE.xload_pre, TE.body, TE.merge, TE.outp, TE.post_gen, TE.post_stats = rope_tabs, stats, xload, xload_pre, body, merge, outp, post_gen, post_stats
            return TE

        for i_, tl_ in enumerate(tiles):
            tl_["idx"] = i_
        objs = [make_tile(tl) for tl in tiles]
        if objs and stage >= 1:
            objs[0].rope_tabs()
            objs[0].stats()
            objs[0].xload_pre()
            objs[0].xload()
            prev_post = None
            for i, t in enumerate(objs):
                nxt = objs[i + 1] if i + 1 < len(objs) else None
                if stage < 2:
                    break
                t.body(prev_post)
                prev_post = None
                if nxt is not None:
                    nxt.rope_tabs()
                    nxt.stats()
                    nxt.xload_pre()
                if stage < 4:
                    break
                t.merge()
                if stage < 5:
                    break
                t.outp([0, 1])
                if nxt is not None:
                    nxt.xload([0, 1])
                t.outp([2])
                if nxt is not None:
                    nxt.xload([2])
                t.outp([3])
                if nxt is not None:
                    nxt.xload([3])
                t.post_stats()
                prev_post = t.post_gen()
            if prev_post is not None:
                for _ in prev_post:
                    pass

        P.finalize()
        P.emit(nc, st)
    return nc, hc


_CACHE = {}


def kernel(x_prompt, x_sample, c_prompt, c_sample, state_conv, state_ret, ada_w, ada_b, norm_pre, norm_post,
           w_in, conv_w, conv_b, w_branch, w_out):
    f = lambda a: np.ascontiguousarray(np.asarray(a, dtype=np.float32))
    x_prompt, x_sample, c_prompt, c_sample = f(x_prompt), f(x_sample), f(c_prompt), f(c_sample)
    state_conv, state_ret = f(state_conv), f(state_ret)
    if "nc" not in _CACHE:
        _CACHE["nc"] = build_program()
    nc, hc = _CACHE["nc"]
    shared = {
        "ada_w": f(ada_w)[0], "ada_b": f(ada_b)[0], "norm_pre": f(norm_pre)[0], "norm_post": f(norm_post)[0],
        "w_in": f(w_in)[0], "conv_w": f(conv_w)[0], "conv_b": f(conv_b)[0], "w_branch": f(w_branch)[0], "w_out": f(w_out)[0],
    }
    for k in CONST_SHAPES:
        shared["k_" + k] = np.ascontiguousarray(hc[k])
    in_maps = []
    for i in range(N_CORES):
        m = dict(shared)
        m["xp"] = x_prompt[2 * i:2 * i + 2].reshape(2 * SEQ, D)
        m["xs"] = x_sample[2 * i:2 * i + 2].reshape(2 * DEC, D)
        m["c4"] = np.ascontiguousarray(np.concatenate([c_prompt[2 * i:2 * i + 2], c_sample[2 * i:2 * i + 2]], 0))
        m["sconv"] = np.ascontiguousarray(state_conv[0, 2 * i:2 * i + 2])
        m["sret"] = np.ascontiguousarray(state_ret[0, 2 * i:2 * i + 2])
        in_maps.append(m)
    res = run_bass_kernel_spmd(nc, in_maps, core_ids=list(range(N_CORES)))
    r = res.results
    y_prompt = np.concatenate([r[i]["yp"].reshape(2, SEQ, D) for i in range(N_CORES)], 0)
    y_sample = np.concatenate([r[i]["ys"].reshape(2, DEC, D) for i in range(N_CORES)], 0)
    ncp = np.concatenate([r[i]["ncp"] for i in range(N_CORES)], 0)[None]
    nrp = np.concatenate([r[i]["nrp"] for i in range(N_CORES)], 0)[None]
    ncs = np.concatenate([r[i]["ncs"] for i in range(N_CORES)], 0)[None]
    nrs = np.concatenate([r[i]["nrs"] for i in range(N_CORES)], 0)[None]
    return (y_prompt.astype(np.float32), y_sample.astype(np.float32), ncp.astype(np.float32), nrp.astype(np.float32),
            ncs.astype(np.float32), nrs.astype(np.float32))
```

```python
import numpy as np
from contextlib import ExitStack
import concourse.bass as bass
import concourse.mybir as mybir
from concourse.bass_utils import run_bass_kernel_spmd

F32 = mybir.dt.float32
BF16 = mybir.dt.bfloat16
ALU = mybir.AluOpType
AF = mybir.ActivationFunctionType


class Op:
    __slots__ = ("eng", "fn", "reads", "writes", "joins", "dma", "deps", "sig", "semv", "idx", "waits")

    def __init__(self, eng, fn, reads, writes, joins, dma):
        self.eng = eng
        self.fn = fn
        self.reads = reads
        self.writes = writes
        self.joins = joins
        self.dma = dma
        self.deps = set()
        self.sig = False
        self.semv = None
        self.waits = []


class Prog:
    ENGS = ("pe", "act", "dve", "pool", "sp")

    def __init__(self):
        self.ops = []
        self.wr = {}
        self.rd = {}
        self.prev = {}
        self.group_final = {("dma", "const")}

    def op(self, eng, fn, reads=(), writes=(), joins=(), dma=None):
        o = Op(eng, fn, tuple(reads), tuple(writes), tuple(joins), dma)
        j = len(self.ops)
        o.idx = j
        me = ("dma", dma) if dma is not None else ("eng", eng)

        def add(ek, i, raw):
            if ek == ("eng", "pe") and eng == "pe" and o.dma is None:
                return
            o.deps.add(i)

        for k in o.reads:
            for ek, i in self.wr.get(k, {}).items():
                add(ek, i, True)
        for k in o.writes:
            pv = {}
            for d in (self.wr.get(k, {}), self.rd.get(k, {})):
                for ek, i in d.items():
                    if pv.get(ek, -1) < i:
                        pv[ek] = i
            self.prev[k] = pv
            for ek, i in pv.items():
                add(ek, i, False)
        for k in o.joins:
            for ek, i in self.prev.get(k, {}).items():
                add(ek, i, False)
        for k in o.reads:
            self.rd.setdefault(k, {})[me] = j
        for k in o.writes:
            self.wr[k] = {me: j}
            self.rd[k] = {}
        for k in o.joins:
            self.wr.setdefault(k, {})[me] = j
        self.ops.append(o)
        return o

    def finalize(self):
        ops = self.ops
        for o in ops:
            for i in o.deps:
                ops[i].sig = True
        cnt = {}
        for o in ops:
            if o.dma is not None:
                key = ("dma", o.dma)
                cnt[key] = cnt.get(key, 0) + 16
                o.semv = (key, cnt[key])
            elif o.sig:
                key = ("eng", o.eng)
                cnt[key] = cnt.get(key, 0) + 1
                o.semv = (key, cnt[key])
        self.final_waits = dict((k, v) for k, v in cnt.items() if k[0] == "dma")
        waited = {e: {} for e in self.ENGS}
        for o in ops:
            need = {}
            for i in o.deps:
                k, v = ops[i].semv
                if k in self.group_final:
                    v = cnt[k]
                if v > need.get(k, 0):
                    need[k] = v
            w = waited[o.eng]
            for k, v in need.items():
                if w.get(k, 0) < v:
                    w[k] = v
                    o.waits.append((k, v))
        waited_vals = {}
        for o in ops:
            for k, v in o.waits:
                if k[0] == "dma":
                    waited_vals.setdefault(k, set()).add(v)
        for o in ops:
            if o.dma is not None:
                k, v = o.semv
                s0 = v - 16
                if s0 > 0 and s0 in waited_vals.get(k, ()):
                    w = waited[o.eng]
                    if not any(kk == k and vv >= s0 for kk, vv in o.waits):
                        o.waits.append((k, s0))
        self.sem_keys = sorted(cnt.keys(), key=str)
        return self

    def emit(self, nc, stack, final_eng="sp"):
        sems = {}
        for n, k in enumerate(self.sem_keys):
            sems[k] = stack.enter_context(nc.semaphore("sem%d" % n))
        block = stack.enter_context(nc.Block())
        by_eng = {e: [o for o in self.ops if o.eng == e] for e in self.ENGS}
        final_waits = self.final_waits

        def run(e, name):
            for o in by_eng[name]:
                for k, v in o.waits:
                    e.wait_ge(sems[k], v)
                ins = o.fn(e)
                if o.semv is not None:
                    k, v = o.semv
                    ins.then_inc(sems[k], 16 if o.dma is not None else 1)
            if name == final_eng:
                for k, v in final_waits.items():
                    e.wait_ge(sems[k], v)

        @block.tensor
        def _(e):
            run(e, "pe")

        @block.scalar
        def _(e):
            run(e, "act")

        @block.vector
        def _(e):
            run(e, "dve")

        @block.gpsimd
        def _(e):
            run(e, "pool")

        @block.sync
        def _(e):
            run(e, "sp")


D = 2048
KC = 16
T = 512
H = 8
DK = 128
DV = 256
SEQ = 2048
DEC = 32
PAST = 1024
EPS = 1e-6
NSLOT = 52
N_CORES = 8
EVAC_MODE = 0


def host_consts():
    c = {}
    inv = 10000.0 ** (-np.arange(0, DK, 2, dtype=np.float32) / DK)

    def rope_tabs(pos):
        ang = pos.astype(np.float32)[None, :] * inv[:, None]
        cos = np.cos(ang).astype(np.float32)
        sin = np.sin(ang).astype(np.float32)
        cosT = np.concatenate([cos, cos], 0)
        sinX = np.concatenate([sin, -sin], 0)
        return np.ascontiguousarray(cosT), np.ascontiguousarray(sinX)

    c["cosP"], c["sinP"] = rope_tabs(np.arange(SEQ))
    ps = PAST + np.arange(DEC)
    c["cosS"], c["sinS"] = rope_tabs(np.concatenate([ps, ps]))
    lg = np.log(1.0 - 2.0 ** (-5.0 - np.arange(H, dtype=np.float64)))
    for L in (64, 32):
        p = np.arange(128)
        j = (p % L)[:, None, None].astype(np.float64)
        i = np.arange(L)[None, None, :].astype(np.float64)
        l3 = lg[None, :, None]
        mask = np.exp(l3 * (np.abs(i - j) - (i + 1.0))) * (DK ** -0.5)
        c["mask%d" % L] = mask.astype(np.float32)
        qd = np.exp(l3 * (i + 1.0)) * np.ones((128, 1, 1))
        c["qdec%d" % L] = qd.astype(np.float32)
        kd = np.exp(lg[None, :] * (L - 1.0 - (p % L)[:, None])) * (DK ** -0.5)
        c["kdec%d" % L] = kd.astype(np.float32)
    c["gpow"] = np.stack([np.exp(lg * 64.0), np.exp(lg * 32.0)]).astype(np.float32)
    c["identf"] = np.eye(128, dtype=np.float32)
    seg = np.zeros((128, 3, 128), np.float32)
    seg[:, 0, :] = 1.0
    seg[:, 1, 0:32] = 1.0
    seg[:, 2, 32:64] = 1.0
    c["oneseg"] = seg
    return c


CONST_SHAPES = {
    "cosP": [128, 2048], "sinP": [128, 2048], "cosS": [128, 64], "sinS": [128, 64],
    "mask64": [128, H, 64], "mask32": [128, H, 32], "qdec64": [128, H, 64], "qdec32": [128, H, 32],
    "kdec64": [128, H], "kdec32": [128, H], "identf": [128, 128], "oneseg": [128, 3, 128],
}


class _Stop(Exception):
    pass


def build_program(npt=4, with_sample=True, nseq=2, stage=99, conv_slots=NSLOT):
    hc = host_consts()
    SEQ = T * max(npt, 1)
    gpow = hc["gpow"]
    nc = bass.Bass("TRN2", target_bir_lowering=False)

    def din(name, shape):
        return nc.dram_tensor(name, shape, F32, kind="ExternalInput").ap()

    def dout(name, shape):
        return nc.dram_tensor(name, shape, F32, kind="ExternalOutput").ap()

    xp = din("xp", [2 * SEQ, D])
    xs = din("xs", [2 * DEC, D])
    c4 = din("c4", [4, D])
    sconv = din("sconv", [2, 2, D])
    sret = din("sret", [2, H, DK, DV])
    ada_w = din("ada_w", [D, 3 * D])
    ada_b = din("ada_b", [3 * D])
    norm_pre = din("norm_pre", [D])
    norm_post = din("norm_post", [D])
    w_in = din("w_in", [D, 18432])
    conv_w = din("conv_w", [3, D])
    conv_b = din("conv_b", [D])
    w_br = din("w_branch", [2, D, D])
    w_out = din("w_out", [D, D])
    cst = {k: din("k_" + k, v) for k, v in CONST_SHAPES.items()}
    yp = dout("yp", [2 * SEQ, D])
    ys = dout("ys", [2 * DEC, D])
    ncp = dout("ncp", [2, 2, D])
    nrp = dout("nrp", [2, H, DK, DV])
    ncs = dout("ncs", [2, 2, D])
    nrs = dout("nrs", [2, H, DK, DV])
    wsc = nc.dram_tensor("wsc", [NSLOT, 128, KC, 512], BF16).ap()

    P = Prog()
    with ExitStack() as st:
        def sb(name, shape, dt=F32):
            return st.enter_context(nc.sbuf_tensor(name, shape, dt))

        def ps(name, shape, dt=F32):
            return st.enter_context(nc.psum_tensor(name, shape, dt))

        xt = [sb("xt%d" % i, [128, D]) for i in range(2)]
        hT = sb("hT", [128, KC, T], BF16)
        wr = [sb("wr%d" % i, [128, KC, 512], BF16) for i in range(3)]
        Ybuf = sb("Ybuf", [128, 32 * 512], BF16)
        yconv = Ybuf[:, 0:8192].rearrange("p (k t) -> p k t", k=KC)
        yret = Ybuf[:, 8192:16384].rearrange("p (k t) -> p k t", k=KC)
        outsb = Ybuf[:].bitcast(F32).rearrange("p (b f) -> p b f", b=4)
        yjunk = Ybuf[:, 0:2048]
        merged = sb("merged", [128, KC, T], BF16)
        hT_f32 = hT[:].rearrange("p k t -> p (k t)").bitcast(F32)
        ggrow = hT_f32[:, 0:2048]
        hjunk = hT[:, 8:12, :].rearrange("p k t -> p (k t)")
        cosT = sb("cosT", [128, T])
        sinX = sb("sinX", [128, T])
        gcs = sb("gcs", [128, T])
        ubuf = sb("ubuf", [128, T + 4])
        cbuf = sb("cbuf", [128, T])
        szc = sb("szc", [128, T])
        tcv = sb("tcv", [128, T])
        rs = sb("rs", [128, T])
        ropeA = sb("ropeA", [128, T])
        ropeB = sb("ropeB", [128, T])
        qT = [sb("qT%d" % i, [128, T], BF16) for i in range(2)]
        kT = [sb("kT%d" % i, [128, T], BF16) for i in range(2)]
        ktok = [sb("ktok%d" % i, [128, 4, DK], BF16) for i in range(2)]
        vtok = [sb("vtok%d" % i, [128, 4, DV], BF16) for i in range(2)]
        szr = [sb("szr%d" % i, [128, 2, T]) for i in range(2)]
        STm = sb("STm", [128, 4, 64], BF16)
        osq = sb("osq", [128, 2, T], BF16)
        rms = sb("rms", [128, T])
        rinv = sb("rinv", [128, T])
        onorm = sb("onorm", [128, 2, T])
        S0 = sb("S0", [128, H, DV])
        Sbf0 = sb("Sbf0", [128, H, DV], BF16)
        S1 = xt[0][:].rearrange("p (h e) -> p h e", h=H)
        Sbf1 = xt[1][:].bitcast(BF16)[:, 0:H * DV].rearrange("p (h e) -> p h e", h=H)
        ta, tb_, mA, mB = gcs, szc, tcv, rs
        diag = [sb("diag%d" % i, [128, 128]) for i in range(2)]
        vecs = sb("vecs", [128, 320])
        stg = [sb("stg%d" % i, [128, 128]) for i in range(3)]
        ostg = sb("ostg", [16, 128])
        cT = vecs[:, 0:64].rearrange("p (s k) -> p s k", s=4)
        scT = sb("scT", [128, 4, KC])
        ada_bT = vecs[:, 64:112]
        g_preT = vecs[:, 112:128]
        g_postT = vecs[:, 128:144]
        conv_wT = vecs[:, 144:192].rearrange("p (j k) -> p j k", j=3)
        conv_bT = vecs[:, 192:208]
        modT = sb("modT", [128, 48, 4])
        gmodT = sb("gmodT", [128, KC, 4])
        ggT = sb("ggT", [128, KC, 4])
        uhP_tk = sb("uhistP", [128, 2, KC])
        uhistP = uhP_tk[:].rearrange("p t k -> p k t")
        uhS_tk = [vecs[:, 256 + 32 * i: 256 + 32 * (i + 1)].rearrange("p (t k) -> p t k", t=2) for i in range(2)]
        uhistS = [u.rearrange("p t k -> p k t") for u in uhS_tk]
        ssb = sb("ssb", [128, 4])
        rmsb = sb("rmsb", [128, 4])
        rstd = sb("rstd", [128, 4])
        ss2 = sb("ss2", [128, 4])
        ss2p = sb("ss2p", [128, 4, 4])
        rms2 = sb("rms2", [128, 4])
        rstd2 = sb("rstd2", [128, 4])
        epsT = sb("epsT", [128, 1])
        identf = sb("identf", [128, 128])
        identb = sb("identb", [128, 128], BF16)
        onesb = sb("onesb", [128, 128], BF16)
        oneseg = sb("oneseg", [128, 3, 128])
        mask = {64: sb("mask64", [128, H, 64]), 32: sb("mask32", [128, H, 32])}
        qdec = {64: sb("qdec64", [128, H, 64]), 32: sb("qdec32", [128, H, 32])}
        kdec = {64: sb("kdec64", [128, H]), 32: sb("kdec32", [128, H])}
        ring = [ps("ring%d" % i, [128, 512]) for i in range(4)]
        oTp = [ps("oT%d" % i, [128, 512]) for i in range(2)]
        STps = ps("STps", [128, 4, 128])
        miscps = ps("miscps", [128, 512])
        ktrps = miscps[:, 0:256].bitcast(BF16).rearrange("p (b d) -> p b d", b=4)

        ring_i = [0]

        def ring_next():
            i = ring_i[0] % 4
            ring_i[0] += 1
            return ring[i], ("ring", i)

        def slot_regions(sid):
            if sid < 16:
                q, half = sid // 2, sid % 2
                if half == 0:
                    return [(w_in[:, 2048 + q * 256: 2048 + q * 256 + 256], 0, 256),
                            (w_in[:, 4096 + q * 256: 4096 + q * 256 + 256], 256, 256)]
                return [(w_in[:, 6144 + q * 256: 6144 + q * 256 + 256], 0, 256),
                        (w_in[:, q * 256: q * 256 + 256], 256, 256)]
            if sid < 32:
                j = sid - 16
                h, b = j // 2, j % 2
                if b == 0:
                    return [(w_in[:, 8192 + h * 128: 8192 + h * 128 + 128], 0, 128),
                            (w_in[:, 9216 + h * 128: 9216 + h * 128 + 128], 128, 128),
                            (w_in[:, 10240 + h * 256: 10240 + h * 256 + 256], 256, 256)]
                return [(w_in[:, 12288 + h * 256: 12288 + h * 256 + 256], 0, 256)]
            if sid < 48:
                q, half = (sid - 32) // 2, (sid - 32) % 2
                if half == 0:
                    return [(w_in[:, 14336 + q * 256: 14336 + q * 256 + 256], 0, 256),
                            (w_br[0][:, q * 256: q * 256 + 256], 256, 256)]
                return [(w_in[:, 16384 + q * 256: 16384 + q * 256 + 256], 0, 256),
                        (w_br[1][:, q * 256: q * 256 + 256], 256, 256)]
            cs = sid - 48
            return [(w_out[:, cs * 512: cs * 512 + 512], 0, 512)]

        def slot_cols(sid):
            return sum(r[2] for r in slot_regions(sid))

        def small_load(dst_ap, src_ap, key, slow=False):
            P.op("sp", lambda e: e.dma_start(out=dst_ap, in_=src_ap, allow_slow_non_contiguous=slow),
                 writes=[key], dma="const")

        P.op("sp", lambda e: e.dma_start(out=stg[0][0:64, :], in_=c4.rearrange("s (k p) -> (s k) p", p=128)), writes=[("stg", 0)], dma="const")
        P.op("sp", lambda e: e.dma_start(out=stg[0][64:112, :], in_=ada_b.rearrange("(m p) -> m p", p=128)), joins=[("stg", 0)], dma="const")
        P.op("sp", lambda e: e.dma_start(out=stg[0][112:128, :], in_=norm_pre.rearrange("(k p) -> k p", p=128)), joins=[("stg", 0)], dma="const")
        P.op("sp", lambda e: e.dma_start(out=stg[1][0:16, :], in_=norm_post.rearrange("(k p) -> k p", p=128)), writes=[("stg", 1)], dma="const")
        P.op("sp", lambda e: e.dma_start(out=stg[1][16:64, :], in_=conv_w.rearrange("j (k p) -> (j k) p", p=128)), joins=[("stg", 1)], dma="const")
        P.op("sp", lambda e: e.dma_start(out=stg[1][64:80, :], in_=conv_b.rearrange("(k p) -> k p", p=128)), joins=[("stg", 1)], dma="const")
        P.op("sp", lambda e: e.dma_start(out=stg[2][0:64, :], in_=sconv.rearrange("i t (k p) -> (i t k) p", p=128)), writes=[("stg", 2)], dma="const")
        small_load(identf[:], cst["identf"], "identf")
        small_load(oneseg[:], cst["oneseg"], "oneseg")
        for L in (64, 32):
            small_load(mask[L][:], cst["mask%d" % L], ("mask", L))
            small_load(qdec[L][:], cst["qdec%d" % L], ("qdec", L))
            small_load(kdec[L][:], cst["kdec%d" % L], ("kdec", L))
        P.op("pool", lambda e: e.memset(epsT[:], EPS), writes=["epsT"])
        P.op("pool", lambda e: e.memset(onesb[:], 1.0), writes=["onesb"])
        P.op("dve", lambda e: e.tensor_copy(out=identb[:], in_=identf[:]), reads=["identf"], writes=["identb"])

        slot_seq = []
        TILE_SLOTS = list(range(16, 32)) + list(range(0, 16)) + list(range(32, NSLOT))
        loaded = [0]
        use_i = [0]

        seen = set()

        def ensure_loaded(upto):
            while loaded[0] <= upto and loaded[0] < len(slot_seq):
                j = loaded[0]
                sid = slot_seq[j]
                n = slot_cols(sid)
                ri = j % 3
                t = wr[ri]
                if sid not in seen:
                    seen.add(sid)
                    for r_i, (src, c0, nn) in enumerate(slot_regions(sid)):
                        kw = dict(writes=[("wr", ri)]) if r_i == 0 else dict(joins=[("wr", ri)])
                        P.op("pool", lambda e, t=t, src=src, c0=c0, nn=nn: e.dma_start(
                            out=t[:, :, c0:c0 + nn], in_=src.rearrange("(k p) n -> p k n", p=128)), dma=("wrc", ri), **kw)
                    P.op("sp", lambda e, t=t, sid=sid, n=n: e.dma_start(out=wsc[sid][:, :, 0:n], in_=t[:, :, 0:n]),
                         reads=[("wr", ri)], writes=[("wsc", sid)], dma=("wb", ri))
                else:
                    P.op("sp", lambda e, t=t, sid=sid, n=n: e.dma_start(out=t[:, :, 0:n], in_=wsc[sid][:, :, 0:n]),
                         reads=[("wsc", sid)], writes=[("wr", ri)], dma=("wr", ri))
                loaded[0] += 1

        def use_slot():
            j = use_i[0]
            use_i[0] += 1
            ensure_loaded(j + 2)
            return wr[j % 3], ("wr", j % 3)

        nrows = [128, 80, 64]
        for i in range(3):
            R, rk = ring_next()
            n_ = nrows[i]
            P.op("pe", lambda e, R=R, i=i, n_=n_: e.transpose(R[:, 0:n_], stg[i][0:n_, :], identf[0:n_, 0:n_]),
                 reads=[("stg", i), "identf"], writes=[rk])
            kw = dict(writes=["vecs"]) if i == 0 else dict(joins=["vecs"])
            if i == 2:
                kw = dict(joins=["vecs"], writes=[("uhistS", 0), ("uhistS", 1)])
            P.op("act", lambda e, R=R, i=i, n_=n_: e.activation(out=vecs[:, 128 * i: 128 * i + n_], in_=R[:, 0:n_], func=AF.Copy),
                 reads=[rk], **kw)
        P.op("act", lambda e: e.activation(out=scT[:], in_=cT[:], func=AF.Silu), reads=["vecs"], writes=["scT"])
        modps, modkey = ring_next()
        scTb = sb("scTb", [128, 4, KC], BF16)
        P.op("dve", lambda e: e.tensor_copy(out=scTb[:], in_=scT[:]), reads=["scT"], writes=["scTb"])
        for mp in range(12):
            aw = wr[mp % 3]
            P.op("pool", lambda e, aw=aw, mp=mp: e.dma_start(out=aw[:], in_=ada_w[:, mp * 512:(mp + 1) * 512].rearrange("(k p) n -> p k n", p=128)),
                 writes=[("wr", mp % 3)], dma=("wrc", mp % 3))
            for sub in range(4):
                mt = 4 * mp + sub
                for kc in range(KC):
                    P.op("pe", lambda e, aw=aw, mt=mt, kc=kc, sub=sub: e.matmul(modps[:, mt * 4:(mt + 1) * 4], lhsT=aw[:, kc, sub * 128:(sub + 1) * 128],
                                                                            rhs=scTb[:, :, kc], start=(kc == 0), stop=(kc == KC - 1)),
                         reads=[("wr", mp % 3), "scTb"], writes=[modkey])
        P.op("dve", lambda e: e.tensor_tensor(out=modT[:], in0=modps[:, 0:192].rearrange("p (m s) -> p m s", s=4),
                                              in1=ada_bT[:, :, None].broadcast_to([128, 48, 4]), op=ALU.add),
             reads=[modkey, "vecs"], writes=["modT"])
        P.op("dve", lambda e: e.scalar_tensor_tensor(out=gmodT[:], in0=modT[:, 16:32, :], scalar=1.0,
                                                     in1=g_preT[:, :, None].broadcast_to([128, KC, 4]), op0=ALU.add, op1=ALU.mult),
             reads=["modT", "vecs"], writes=["gmodT"])
        P.op("dve", lambda e: e.scalar_tensor_tensor(out=ggT[:], in0=modT[:, 32:48, :], scalar=0.5,
                                                     in1=g_postT[:, :, None].broadcast_to([128, KC, 4]), op0=ALU.mult, op1=ALU.mult),
             reads=["modT", "vecs"], writes=["ggT"])

        tiles = []
        for s in range(nseq if npt > 0 else 0):
            for t0 in range(0, SEQ, T):
                tiles.append(dict(kind="p", ntok=T, L=64, xsrc=xp, ydst=yp, row0=s * SEQ + t0, pos0=t0, seq=s,
                                  first=(t0 == 0), last=(t0 == SEQ - T),
                                  segs=[dict(col0=0, n=T, mod=s)], blocks=[(b * 128, 128) for b in range(4)]))
        if with_sample:
          tiles.append(dict(kind="s", ntok=2 * DEC, L=32, xsrc=xs, ydst=ys, row0=0, pos0=0, seq=0, first=True, last=True,
                          segs=[dict(col0=0, n=DEC, mod=2), dict(col0=DEC, n=DEC, mod=3)], blocks=[(0, 64)]))
        for _ in tiles:
            slot_seq.extend(TILE_SLOTS)

        def HK(kc):
            return ("hT", kc)

        class TileEmit:
            pass

        def make_tile(tl):
            ntok = tl["ntok"]
            L = tl["L"]
            blocks = tl["blocks"]
            segs = tl["segs"]
            xsrc, ydst, row0 = tl["xsrc"], tl["ydst"], tl["row0"]
            nblk = len(blocks)
            isS = tl["kind"] == "s"
            if isS:
                chunks = [(0, 32, 0, 0, 0), (32, 32, 0, 32, 1)]
            else:
                chunks = [(c * 64, 64, c // 2, (c % 2) * 64, 0) for c in range(8)]
            junkb = onorm[:].rearrange("p a t -> p (a t)").bitcast(BF16)
            mgf = merged[:].rearrange("p k t -> p (k t)").bitcast(F32)
            ggh = [mgf[:, 1024 * i: 1024 * (i + 1)] for i in range(2)]
            TE = TileEmit()

            def rope_tabs():
                if isS:
                    P.op("sp", lambda e: e.dma_start(out=cosT[:, 0:64], in_=cst["cosS"]), writes=["cosT"], dma="ropec")
                    P.op("sp", lambda e: e.dma_start(out=sinX[:, 0:64], in_=cst["sinS"]), writes=["sinX"], dma="ropes")
                else:
                    p0 = tl["pos0"]
                    P.op("sp", lambda e, p0=p0: e.dma_start(out=cosT[:], in_=cst["cosP"][:, p0:p0 + T]), writes=["cosT"], dma="ropec")
                    P.op("sp", lambda e, p0=p0: e.dma_start(out=sinX[:], in_=cst["sinP"][:, p0:p0 + T]), writes=["sinX"], dma="ropes")

            def stats():
                P.op("pool", lambda e: e.memset(ssb[:], 0.0), writes=["ssb"])
                for bi, (b0, nb) in enumerate(blocks):
                    x_ = xt[bi % 2]
                    P.op("act", lambda e, x_=x_, b0=b0, nb=nb: e.dma_start(out=x_[0:nb, :], in_=xsrc[row0 + b0: row0 + b0 + nb, :]),
                         writes=[("xt", bi % 2)], dma=("xt", bi % 2))
                    P.op("act", lambda e, x_=x_, nb=nb, bi=bi: e.activation(out=junkb[0:nb, :], in_=x_[0:nb, :], func=AF.Square,
                                                                            accum_out=ssb[0:nb, bi:bi + 1]),
                         reads=[("xt", bi % 2), "ssb"], writes=["ssb", "onorm"])
                P.op("act", lambda e: e.activation(out=rmsb[:, 0:nblk], in_=ssb[:, 0:nblk], func=AF.Sqrt, scale=1.0 / D, bias=epsT[:]),
                     reads=["ssb", "epsT"], writes=["rmsb"])
                P.op("dve", lambda e: e.reciprocal(out=rstd[:, 0:nblk], in_=rmsb[:, 0:nblk]), reads=["rmsb"], writes=["rstd"])

            szv = [szr[i][:].rearrange("p a t -> p (a t)") for i in range(2)]

            def xparts(bi):
                if bi == 2:
                    return [(szv[0], ("szr", 0), 0, ("x2", 0)), (szv[1], ("szr", 1), 1024, ("x2", 1))]
                return [(xt[bi % 2], ("xt", bi % 2), 0, ("xt", bi % 2))]

            def x_dma_scale(bi):
                b0, nb = blocks[bi]
                for (xa, xk, c0, dk) in xparts(bi):
                    nc_ = xa.shape[-1]
                    P.op("act", lambda e, xa=xa, b0=b0, nb=nb, c0=c0, nc_=nc_: e.dma_start(out=xa[0:nb, :], in_=xsrc[row0 + b0: row0 + b0 + nb, c0:c0 + nc_]),
                         writes=[xk], dma=dk)
                    P.op("dve", lambda e, xa=xa, nb=nb, bi=bi: e.tensor_scalar(out=xa[0:nb, :], in0=xa[0:nb, :], scalar1=rstd[0:nb, bi:bi + 1],
                                                                               scalar2=None, op0=ALU.mult),
                         reads=[xk, "rstd"], writes=[xk])

            def xload_pre():
                for bi in range(min(3, nblk)):
                    x_dma_scale(bi)

            hw_first = [True] * KC

            def xload(bis=None):
                for bi, (b0, nb) in enumerate(blocks):
                    if bis is not None and bi not in bis:
                        continue
                    parts = xparts(bi)
                    for g in range(4):
                        R, rk = ring_next()
                        for j in range(4):
                            kc = 4 * g + j
                            xa, xk, c0, _dk = parts[(kc * 128) // 1024] if len(parts) == 2 else parts[0]
                            lo_c = kc * 128 - c0
                            P.op("pe", lambda e, R=R, xa=xa, nb=nb, j=j, lo_c=lo_c: e.transpose(R[:, j * 128: j * 128 + nb], xa[0:nb, lo_c:lo_c + 128],
                                                                                              identf[0:nb, 0:nb]),
                                 reads=[xk, "identf"], writes=[rk])
                        for j in range(4):
                            kc = 4 * g + j
                            for sg in segs:
                                lo = max(sg["col0"], b0)
                                hi = min(sg["col0"] + sg["n"], b0 + nb)
                                if hi <= lo:
                                    continue
                                md = sg["mod"]
                                kw = dict(writes=[HK(kc)]) if hw_first[kc] else dict(joins=[HK(kc)])
                                hw_first[kc] = False
                                src = R[:, j * 128 + (lo - b0): j * 128 + (hi - b0)]
                                dst = hT[:, kc, lo:hi]
                                if g % 2 == 0:
                                    P.op("act", lambda e, src=src, dst=dst, kc=kc, md=md: e.activation(
                                        out=dst, in_=src, func=AF.Identity, scale=gmodT[:, kc, md:md + 1], bias=modT[:, kc, md:md + 1]),
                                        reads=[rk, "gmodT", "modT"], **kw)
                                else:
                                    P.op("dve", lambda e, src=src, dst=dst, kc=kc, md=md: e.tensor_scalar(
                                        out=dst, in0=src, scalar1=gmodT[:, kc, md:md + 1], scalar2=modT[:, kc, md:md + 1],
                                        op0=ALU.mult, op1=ALU.add),
                                        reads=[rk, "gmodT", "modT"], **kw)
                    if bi == 1 and nblk > 3:
                        x_dma_scale(3)

            def unit_fm_g(wt, wk, c0, src3, srckey_fn):
                R, rk = ring_next()
                for kc in range(KC):
                    P.op("pe", lambda e, R=R, wt=wt, c0=c0, kc=kc, src3=src3: e.matmul(
                        R[:, 0:ntok], lhsT=wt[:, kc, c0:c0 + 128], rhs=src3[:, kc, 0:ntok], start=(kc == 0), stop=(kc == KC - 1)),
                        reads=[wk, srckey_fn(kc)], writes=[rk])
                    if kc == KC // 2 - 1:
                        yield
                return R, rk

            def unit_fm(wt, wk, c0, src3, srckey_fn):
                g = unit_fm_g(wt, wk, c0, src3, srckey_fn)
                try:
                    while True:
                        next(g)
                except StopIteration as ex:
                    return ex.value

            cbufs = [cbuf, ropeA]
            cbkeys = [lambda si: ("cbufA", si), lambda si: "ropeA"]

            def conv_gen(q):
                if q == 0 and tl["kind"] == "p" and tl["first"]:
                    P.op("pool", lambda e: e.memset(uhP_tk[:], 0.0), writes=["uhistP"])
                wt, wk = use_slot()
                for i in range(2):
                    cb = 2 * q + i
                    cbf, cbk = cbufs[i], cbkeys[i]
                    Rgc, kgc = yield from unit_fm_g(wt, wk, i * 128, hT, HK)
                    P.op("act", lambda e, Rgc=Rgc: e.activation(out=gcs[:, 0:ntok], in_=Rgc[:, 0:ntok], func=AF.Copy),
                         reads=[kgc], writes=["gcs"])
                    yield
                    Rgu, kgu = yield from unit_fm_g(wt, wk, 256 + i * 128, hT, HK)
                    for si, sg in enumerate(segs):
                        base = sg["col0"] + 2 * si
                        n = sg["n"]
                        c0 = sg["col0"]
                        uh, uhk = (uhistS[si], ("uhistS", si)) if isS else (uhistP, "uhistP")
                        P.op("pool", lambda e, uh=uh, base=base, cb=cb: e.tensor_copy(out=ubuf[:, base:base + 2], in_=uh[:, cb, :]),
                             reads=[uhk], writes=[("ubh", si)])
                        P.op("dve", lambda e, Rgu=Rgu, base=base, n=n, c0=c0: e.tensor_tensor(
                            out=ubuf[:, base + 2: base + 2 + n], in0=Rgu[:, c0:c0 + n], in1=gcs[:, c0:c0 + n], op=ALU.mult),
                            reads=[kgu, "gcs"], writes=[("ubd", si)])
                        P.op("pool", lambda e, uh=uh, base=base, n=n, cb=cb: e.tensor_copy(out=uh[:, cb, :], in_=ubuf[:, base + n: base + n + 2]),
                             reads=[("ubd", si)], writes=[uhk])
                        P.op("act", lambda e, base=base, n=n, c0=c0, cb=cb, cbf=cbf: e.activation(
                            out=cbf[:, c0:c0 + n], in_=ubuf[:, base + 2: base + 2 + n], func=AF.Identity,
                            scale=conv_wT[:, 2, cb:cb + 1], bias=conv_bT[:, cb:cb + 1]),
                            reads=[("ubd", si), "vecs"], writes=[cbk(si)])
                        P.op("dve", lambda e, base=base, n=n, c0=c0, cb=cb, cbf=cbf: e.scalar_tensor_tensor(
                            out=cbf[:, c0:c0 + n], in0=ubuf[:, base + 1: base + 1 + n], scalar=conv_wT[:, 1, cb:cb + 1],
                            in1=cbf[:, c0:c0 + n], op0=ALU.mult, op1=ALU.add),
                            reads=[("ubd", si), ("ubh", si), "vecs", cbk(si)], writes=[cbk(si)])
                        P.op("dve", lambda e, base=base, n=n, c0=c0, cb=cb, cbf=cbf: e.scalar_tensor_tensor(
                            out=cbf[:, c0:c0 + n], in0=ubuf[:, base: base + n], scalar=conv_wT[:, 0, cb:cb + 1],
                            in1=cbf[:, c0:c0 + n], op0=ALU.mult, op1=ALU.add),
                            reads=[("ubd", si), ("ubh", si), "vecs", cbk(si)], writes=[cbk(si)])
                    yield
                wt2, wk2 = use_slot()
                for i in range(2):
                    cb = 2 * q + i
                    cbf, cbk = cbufs[i], cbkeys[i]
                    Rzc, kzc = yield from unit_fm_g(wt2, wk2, i * 128, hT, HK)
                    P.op("act", lambda e, Rzc=Rzc: e.activation(out=szc[:, 0:ntok], in_=Rzc[:, 0:ntok], func=AF.Silu),
                         reads=[kzc], writes=["szc"])
                    yield
                    Rgb, kgb = yield from unit_fm_g(wt2, wk2, 256 + i * 128, hT, HK)
                    P.op("dve", lambda e, Rgb=Rgb, cbf=cbf: e.tensor_tensor(out=tcv[:, 0:ntok], in0=Rgb[:, 0:ntok], in1=cbf[:, 0:ntok], op=ALU.mult),
                         reads=[kgb] + [cbk(si) for si in range(len(segs))], writes=["tcv"])
                    P.op("pool", lambda e, cb=cb: e.tensor_tensor(out=yconv[:, cb, 0:ntok], in0=tcv[:, 0:ntok], in1=szc[:, 0:ntok], op=ALU.mult),
                         reads=["tcv", "szc"], writes=[("Y", cb)])
                    yield

            def conv_out(src_tk, skey, dst_seq):
                for t_ in range(2):
                    R, rk = ring_next()
                    P.op("pe", lambda e, R=R, t_=t_: e.transpose(R[0:KC, 0:128], src_tk[:, t_, :], identf[:, :]),
                         reads=[skey, "identf"], writes=[rk])
                    P.op("act", lambda e, R=R: e.activation(out=ostg[:, :], in_=R[0:KC, 0:128], func=AF.Copy), reads=[rk], writes=["ostg"])
                    P.op("sp", lambda e, t_=t_: e.dma_start(out=dst_seq[t_].rearrange("(k p) -> k p", p=128), in_=ostg[:, :]),
                         reads=["ostg"], dma="outs")

            def conv_state_out():
                if isS:
                    for si in range(2):
                        conv_out(uhS_tk[si], ("uhistS", si), ncs[si])
                elif tl["last"]:
                    conv_out(uhP_tk[:], "uhistP", ncp[tl["seq"]])

            def Skeys(slot, h):
                if slot == 0:
                    return ("S", h), ("Sbf", h)
                return ("xt", 0), ("xt", 1)

            Ssl = [S0[:], S1]
            Sbsl = [Sbf0[:], Sbf1]

            def state_init():
                if isS:
                    for sl in range(2):
                        P.op("sp", lambda e, sl=sl: e.dma_start(out=Ssl[sl], in_=sret[sl].rearrange("h d e -> d h e")),
                             writes=[Skeys(sl, h)[0] for h in range(H)] if sl == 0 else [("xt", 0)], dma=("sin", sl))
                        P.op("act", lambda e, sl=sl: e.activation(out=Sbsl[sl], in_=Ssl[sl], func=AF.Copy),
                             reads=[Skeys(sl, h)[0] for h in range(H)] if sl == 0 else [("xt", 0)],
                             writes=[Skeys(sl, h)[1] for h in range(H)] if sl == 0 else [("xt", 1)])
                elif tl["first"]:
                    P.op("pool", lambda e: e.memset(S0[:], 0.0), writes=[("S", h) for h in range(H)])
                    P.op("pool", lambda e: e.memset(Sbf0[:], 0.0), writes=[("Sbf", h) for h in range(H)])

            def rope_to(dst, dkey):
                P.op("dve", lambda e: e.tensor_tensor(out=ropeA[:, 0:ntok], in0=rs[:, 0:ntok], in1=cosT[:, 0:ntok], op=ALU.mult),
                     reads=["rs", "cosT"], writes=["ropeA"])
                P.op("dve", lambda e: e.tensor_tensor(out=ropeB[0:64, 0:ntok], in0=rs[64:128, 0:ntok], in1=sinX[64:128, 0:ntok], op=ALU.mult),
                     reads=["rs", "sinX"], writes=["ropeB0"])
                P.op("dve", lambda e: e.tensor_tensor(out=ropeB[64:128, 0:ntok], in0=rs[0:64, 0:ntok], in1=sinX[0:64, 0:ntok], op=ALU.mult),
                     reads=["rs", "sinX"], writes=["ropeB1"])
                P.op("dve", lambda e: e.tensor_tensor(out=dst[:, 0:ntok], in0=ropeA[:, 0:ntok], in1=ropeB[:, 0:ntok], op=ALU.add),
                     reads=["ropeA", "ropeB0", "ropeB1"], writes=[dkey])

            def proj_gen(h):
                pb = h % 2
                wt, wk = use_slot()
                nch = ntok // L
                Rq, kq = yield from unit_fm_g(wt, wk, 0, hT, HK)
                P.op("dve", lambda e, Rq=Rq, h=h: e.tensor_tensor(
                    out=rs[:, 0:ntok].rearrange("p (c l) -> p c l", l=L), in0=Rq[:, 0:ntok].rearrange("p (c l) -> p c l", l=L),
                    in1=qdec[L][:, h, None, :].broadcast_to([128, nch, L]), op=ALU.mult),
                    reads=[kq, ("qdec", L)], writes=["rs"])
                rope_to(qT[pb], ("qT", pb))
                yield
                Rk, kk = yield from unit_fm_g(wt, wk, 128, hT, HK)
                P.op("act", lambda e, Rk=Rk: e.activation(out=rs[:, 0:ntok], in_=Rk[:, 0:ntok], func=AF.Copy), reads=[kk], writes=["rs"])
                rope_to(kT[pb], ("kT", pb))
                yield
                for g0 in range(0, nblk, 2):
                    R, rk = ring_next()
                    gb = blocks[g0:g0 + 2]
                    for j, (b0, nb) in enumerate(gb):
                        for kc in range(KC):
                            P.op("pe", lambda e, R=R, j=j, b0=b0, nb=nb, kc=kc, wt=wt: e.matmul(
                                R[0:nb, j * 256:(j + 1) * 256], lhsT=hT[:, kc, b0:b0 + nb], rhs=wt[:, kc, 256:512],
                                start=(kc == 0), stop=(kc == KC - 1)),
                                reads=[wk, HK(kc)], writes=[rk])
                        if j == 0 and len(gb) > 1:
                            yield
                    nbm = gb[0][1]
                    ng = len(gb)
                    kw = dict(writes=[("vtok", pb)]) if g0 == 0 else dict(joins=[("vtok", pb)])
                    P.op("act", lambda e, R=R, g0=g0, ng=ng, nbm=nbm, pb=pb: e.activation(
                        out=vtok[pb][0:nbm, g0:g0 + ng, :], in_=R[0:nbm, 0:ng * 256].rearrange("p (b e) -> p b e", e=256), func=AF.Copy),
                        reads=[rk], **kw)
                    yield
                for bi, (b0, nb) in enumerate(blocks):
                    P.op("pe", lambda e, bi=bi, b0=b0, nb=nb, pb=pb: e.transpose(ktrps[0:nb, bi, :], kT[pb][:, b0:b0 + nb], identb[:, :]),
                         reads=[("kT", pb), "identb"], writes=["ktrps"])
                nbm = blocks[0][1]
                P.op("dve", lambda e, h=h, pb=pb, nbm=nbm: e.tensor_scalar(
                    out=ktok[pb][0:nbm, 0:nblk, :], in0=ktrps[0:nbm, 0:nblk, :], scalar1=kdec[L][0:nbm, h:h + 1], scalar2=None, op0=ALU.mult),
                    reads=["ktrps", ("kdec", L)], writes=[("ktok", pb)])
                wt2, wk2 = use_slot()
                for m in range(2):
                    R, rk = yield from unit_fm_g(wt2, wk2, m * 128, hT, HK)
                    kw = dict(writes=[("szr", pb)]) if m == 0 else dict(joins=[("szr", pb)])
                    P.op("act", lambda e, R=R, m=m, pb=pb: e.activation(out=szr[pb][:, m, 0:ntok], in_=R[:, 0:ntok], func=AF.Silu),
                         reads=[rk], **kw)
                    yield
                for bi, (b0, nb) in enumerate(blocks):
                    P.op("pe", lambda e, bi=bi, b0=b0, nb=nb, pb=pb: e.matmul(
                        STps[0:nb, bi, 0:nb], lhsT=kT[pb][:, b0:b0 + nb], rhs=qT[pb][:, b0:b0 + nb], start=True, stop=True),
                        reads=[("kT", pb), ("qT", pb)], writes=["STps"])
                nper = 2
                for po_i in range(nper):
                    po = po_i * L
                    kw = dict(writes=["STm"]) if po_i == 0 else dict(joins=["STm"])
                    P.op("dve", lambda e, po=po, h=h: e.tensor_tensor(
                        out=STm[po:po + L, 0:nblk, 0:L], in0=STps[po:po + L, 0:nblk, po:po + L],
                        in1=mask[L][po:po + L, h, None, :].broadcast_to([L, nblk, L]), op=ALU.mult),
                        reads=["STps", ("mask", L)], **kw)
                yield

            def ret_gen(h):
                pb = h % 2
                gLh = float(gpow[1 if isS else 0][h])
                for (c0, Lc, bi, po, sl) in chunks:
                    sk, sbk = Skeys(sl, h)
                    S_, Sb_ = Ssl[sl], Sbsl[sl]
                    for ec in range(2):
                        P.op("pe", lambda e, ec=ec, c0=c0, Lc=Lc, bi=bi, po=po, pb=pb: e.matmul(
                            oTp[ec][:, c0:c0 + Lc], lhsT=vtok[pb][po:po + Lc, bi, ec * 128:(ec + 1) * 128], rhs=STm[po:po + Lc, bi, 0:Lc],
                            start=True, stop=False),
                            reads=[("vtok", pb), "STm"], writes=[("oT", ec)])
                        P.op("pe", lambda e, ec=ec, c0=c0, Lc=Lc, Sb_=Sb_, h=h, pb=pb: e.matmul(
                            oTp[ec][:, c0:c0 + Lc], lhsT=Sb_[:, h, ec * 128:(ec + 1) * 128], rhs=qT[pb][:, c0:c0 + Lc],
                            start=False, stop=True),
                            reads=[sbk, ("qT", pb)], writes=[("oT", ec)])
                    Rd, rdk = ring_next()
                    P.op("pe", lambda e, Rd=Rd, Lc=Lc, bi=bi, po=po, pb=pb: e.matmul(
                        Rd[:, 0:DV], lhsT=ktok[pb][po:po + Lc, bi, :], rhs=vtok[pb][po:po + Lc, bi, :], start=True, stop=True),
                        reads=[("ktok", pb), ("vtok", pb)], writes=[rdk])
                    P.op("dve", lambda e, Rd=Rd, S_=S_, Sb_=Sb_, h=h, gLh=gLh: e.scalar_tensor_tensor(
                        out=Sb_[:, h, :], in0=S_[:, h, :], scalar=gLh, in1=Rd[:, 0:DV], op0=ALU.mult, op1=ALU.add),
                        reads=[sk, rdk], writes=[sbk])
                    P.op("dve", lambda e, Rd=Rd, S_=S_, h=h, gLh=gLh: e.scalar_tensor_tensor(
                        out=S_[:, h, :], in0=S_[:, h, :], scalar=gLh, in1=Rd[:, 0:DV], op0=ALU.mult, op1=ALU.add),
                        reads=[sk, rdk], writes=[sk])
                    yield
                for ec in range(2):
                    kw = dict(writes=["osq"]) if ec == 0 else dict(joins=["osq"])
                    P.op("act", lambda e, ec=ec: e.activation(out=osq[:, ec, 0:ntok], in_=oTp[ec][:, 0:ntok], func=AF.Square),
                         reads=[("oT", ec)], **kw)
                yield
                R, rk = ring_next()
                for ec in range(2):
                    P.op("pe", lambda e, R=R, ec=ec: e.matmul(R[:, 0:ntok], lhsT=onesb[:, :], rhs=osq[:, ec, 0:ntok], start=(ec == 0), stop=(ec == 1)),
                         reads=["osq", "onesb"], writes=[rk])
                P.op("act", lambda e, R=R: e.activation(out=rms[:, 0:ntok], in_=R[:, 0:ntok], func=AF.Ln, scale=1.0 / DV, bias=epsT[:]),
                     reads=[rk, "epsT"], writes=["rms"])
                P.op("act", lambda e: e.activation(out=rinv[:, 0:ntok], in_=rms[:, 0:ntok], func=AF.Exp, scale=-0.5), reads=["rms"], writes=["rinv"])
                for ec in range(2):
                    kw = dict(writes=["onorm"]) if ec == 0 else dict(joins=["onorm"])
                    P.op("dve", lambda e, ec=ec: e.tensor_tensor(out=onorm[:, ec, 0:ntok], in0=oTp[ec][:, 0:ntok], in1=rinv[:, 0:ntok], op=ALU.mult),
                         reads=[("oT", ec), "rinv"], **kw)
                P.op("pool", lambda e, h=h, pb=pb: e.tensor_tensor(out=yret[:, 2 * h:2 * h + 2, 0:ntok], in0=onorm[:, :, 0:ntok],
                                                                    in1=szr[pb][:, :, 0:ntok], op=ALU.mult),
                     reads=["onorm", ("szr", pb)], writes=[("Y", 16 + 2 * h), ("Y", 16 + 2 * h + 1)])
                yield

            def interleave(main, side):
                dm = ds = False
                while not (dm and ds):
                    if not dm:
                        try:
                            next(main)
                        except StopIteration:
                            dm = True
                    if not ds:
                        try:
                            next(side)
                        except StopIteration:
                            ds = True

            def chain(*gens):
                for g in gens:
                    for _ in g:
                        yield

            def empty():
                return
                yield

            def body(prev_post=None):
                if prev_post is None:
                    prev_post = empty()
                if isS:
                    for _ in prev_post:
                        pass
                    prev_post = empty()
                state_init()
                for h in range(H):
                    interleave(proj_gen(h), ret_gen(h - 1) if h >= 1 else prev_post)
                interleave(chain(*[conv_gen(q) for q in range(KC // 2)]), ret_gen(H - 1))
                conv_state_out()
                if isS:
                    for sl in range(2):
                        P.op("sp", lambda e, sl=sl: e.dma_start(out=nrs[sl].rearrange("h d e -> d h e"), in_=Ssl[sl]),
                             reads=[Skeys(sl, h)[0] for h in range(H)] if sl == 0 else [("xt", 0)], dma="outs")
                elif tl["last"]:
                    sq = tl["seq"]
                    P.op("sp", lambda e, sq=sq: e.dma_start(out=nrp[sq].rearrange("h d e -> d h e"), in_=S0[:]),
                         reads=[("S", h) for h in range(H)], dma="outs")

            def merge():
                mAb = [tcv, ropeA]
                mAk = ["tcv", "ropeA"]
                for q in range(KC // 2):
                    wt, wk = use_slot()
                    for i in range(2):
                        R1, k1 = unit_fm(wt, wk, i * 128, hT, HK)
                        P.op("act", lambda e, R1=R1: e.activation(out=ta[:, 0:ntok], in_=R1[:, 0:ntok], func=AF.Tanh, scale=0.5), reads=[k1], writes=["gcs"])
                        R2, k2 = unit_fm(wt, wk, 256 + i * 128, yconv, lambda kc: ("Y", kc))
                        P.op("dve", lambda e, R2=R2, i=i: e.scalar_tensor_tensor(out=mAb[i][:, 0:ntok], in0=ta[:, 0:ntok], scalar=1.0, in1=R2[:, 0:ntok],
                                                                                 op0=ALU.add, op1=ALU.mult), reads=["gcs", k2], writes=[mAk[i]])
                    wt2, wk2 = use_slot()
                    for i in range(2):
                        m = 2 * q + i
                        R3, k3 = unit_fm(wt2, wk2, i * 128, hT, HK)
                        P.op("act", lambda e, R3=R3: e.activation(out=tb_[:, 0:ntok], in_=R3[:, 0:ntok], func=AF.Tanh, scale=0.5), reads=[k3], writes=["szc"])
                        R4, k4 = unit_fm(wt2, wk2, 256 + i * 128, yret, lambda kc: ("Y", 16 + kc))
                        P.op("dve", lambda e, R4=R4: e.scalar_tensor_tensor(out=mB[:, 0:ntok], in0=tb_[:, 0:ntok], scalar=1.0, in1=R4[:, 0:ntok],
                                                                            op0=ALU.add, op1=ALU.mult), reads=["szc", k4], writes=["rs"])
                        P.op("pool", lambda e, m=m, i=i: e.tensor_tensor(out=merged[:, m, 0:ntok], in0=mAb[i][:, 0:ntok], in1=mB[:, 0:ntok], op=ALU.add),
                             reads=[mAk[i], "rs"], writes=[("mg", m)])

            def outp(cs_list):
                if 0 in cs_list:
                    P.op("pool", lambda e: e.memset(ss2p[:], 0.0), writes=["ss2p"])
                for cs in cs_list:
                    wt, wk = use_slot()
                    for bi, (b0, nb) in enumerate(blocks):
                        R, rk = ring_next()
                        for kc in range(KC):
                            P.op("pe", lambda e, R=R, b0=b0, nb=nb, kc=kc, wt=wt: e.matmul(
                                R[0:nb, :], lhsT=merged[:, kc, b0:b0 + nb], rhs=wt[:, kc, 0:512], start=(kc == 0), stop=(kc == KC - 1)),
                                reads=[wk, ("mg", kc)], writes=[rk])
                        P.op("act", lambda e, R=R, nb=nb, bi=bi, cs=cs: e.activation(out=outsb[0:nb, bi, cs * 512:(cs + 1) * 512], in_=R[0:nb, :], func=AF.Copy),
                             reads=[rk], writes=[("Y", 8 * bi + 2 * cs), ("Y", 8 * bi + 2 * cs + 1)])
                        P.op("act", lambda e, R=R, nb=nb, bi=bi, cs=cs: e.activation(out=junkb[0:nb, 0:512], in_=R[0:nb, :], func=AF.Square,
                                                                                 accum_out=ss2p[0:nb, bi, cs:cs + 1]),
                             reads=[rk, "ss2p"], writes=["ss2p", "onorm"])

            def ggbuild_gen():
                nbk = blocks[0][1]
                di = 0
                for g in range(4):
                    R, rk = ring_next()
                    for j in range(4):
                        kc = 4 * g + j
                        for si, sg in enumerate(segs):
                            md = sg["mod"]
                            dg = diag[di % 2]
                            dkey = ("diag", di % 2)
                            di += 1
                            P.op("dve", lambda e, dg=dg, kc=kc, md=md: e.tensor_scalar(out=dg[:], in0=identf[:], scalar1=ggT[:, kc, md:md + 1],
                                                                                      scalar2=None, op0=ALU.mult),
                                 reads=["identf", "ggT"], writes=[dkey])
                            sidx = 0 if not isS else 1 + si
                            P.op("pe", lambda e, R=R, j=j, dg=dg, sidx=sidx, si=si: e.matmul(
                                R[0:nbk, j * 128:(j + 1) * 128], lhsT=oneseg[:, sidx, 0:nbk], rhs=dg[:], start=(si == 0), stop=(si == len(segs) - 1)),
                                reads=[dkey, "oneseg"], writes=[rk])
                    gh = ggh[g // 2]
                    P.op("act", lambda e, R=R, g=g, gh=gh: e.activation(out=gh[0:nbk, (g % 2) * 512:(g % 2 + 1) * 512], in_=R[0:nbk, :], func=AF.Copy),
                         reads=[rk], writes=[("mg", 2 * g), ("mg", 2 * g + 1)])
                    yield

            def post_stats():
                P.op("dve", lambda e: e.tensor_reduce(out=ss2[:, 0:nblk], in_=ss2p[:, 0:nblk, :], axis=mybir.AxisListType.X, op=ALU.add),
                     reads=["ss2p"], writes=["ss2"])
                P.op("act", lambda e: e.activation(out=rms2[:, 0:nblk], in_=ss2[:, 0:nblk], func=AF.Sqrt, scale=1.0 / (4.0 * D), bias=epsT[:]),
                     reads=["ss2", "epsT"], writes=["rms2"])
                P.op("dve", lambda e: e.reciprocal(out=rstd2[:, 0:nblk], in_=rms2[:, 0:nblk]), reads=["rms2"], writes=["rstd2"])

            def post_gen():
                yield from ggbuild_gen()
                for bi, (b0, nb) in enumerate(blocks):
                    x_ = xt[bi % 2]
                    yk = [("Y", 8 * bi + j) for j in range(8)]
                    P.op("act", lambda e, x_=x_, b0=b0, nb=nb: e.dma_start(out=x_[0:nb, :], in_=xsrc[row0 + b0: row0 + b0 + nb, :]),
                         writes=[("xt", bi % 2)], dma=("xt", bi % 2))
                    for hf in range(2):
                        ykh = yk[4 * hf: 4 * hf + 4]
                        P.op("dve", lambda e, nb=nb, bi=bi, hf=hf: e.scalar_tensor_tensor(
                            out=outsb[0:nb, bi, hf * 1024:(hf + 1) * 1024], in0=outsb[0:nb, bi, hf * 1024:(hf + 1) * 1024],
                            scalar=rstd2[0:nb, bi:bi + 1], in1=ggh[hf][0:nb, :], op0=ALU.mult, op1=ALU.mult),
                            reads=ykh + ["rstd2"] + [("mg", 4 * hf + q_) for q_ in range(4)], writes=ykh)
                    P.op("pool", lambda e, nb=nb, bi=bi, x_=x_: e.tensor_tensor(out=outsb[0:nb, bi, :], in0=outsb[0:nb, bi, :], in1=x_[0:nb, :], op=ALU.add),
                         reads=yk + [("xt", bi % 2)], writes=yk)
                    P.op("pool", lambda e, nb=nb, bi=bi, b0=b0: e.dma_start(out=ydst[row0 + b0: row0 + b0 + nb, :], in_=outsb[0:nb, bi, :]),
                         reads=yk, dma="outy")
                    yield

            TE.rope_tabs, TE.stats, TE.xload, TE.xload_pre, TE.body, TE.merge, TE.outp, TE.post_gen, TE.post_stats = rope_tabs, stats, xload, xload_pre, body, merge, outp, post_gen, post_stats
            return TE

        objs = [make_tile(tl) for tl in tiles]
        if objs and stage >= 1:
            objs[0].rope_tabs()
            objs[0].stats()
            objs[0].xload_pre()
            objs[0].xload()
            prev_post = None
            for i, t in enumerate(objs):
                nxt = objs[i + 1] if i + 1 < len(objs) else None
                if stage < 2:
                    break
                t.body(prev_post)
                prev_post = None
                if nxt is not None:
                    nxt.rope_tabs()
                    nxt.stats()
                    nxt.xload_pre()
                if stage < 4:
                    break
                t.merge()
                if stage < 5:
                    break
                t.outp([0, 1])
                if nxt is not None:
                    nxt.xload([0, 1])
                t.outp([2])
                if nxt is not None:
                    nxt.xload([2])
                t.outp([3])
                if nxt is not None:
                    nxt.xload([3])
                t.post_stats()
                prev_post = t.post_gen()
            if prev_post is not None:
                for _ in prev_post:
                    pass

        P.finalize()
        P.emit(nc, st)
    return nc, hc


_CACHE = {}


def kernel(x_prompt, x_sample, c_prompt, c_sample, state_conv, state_ret, ada_w, ada_b, norm_pre, norm_post,
           w_in, conv_w, conv_b, w_branch, w_out):
    f = lambda a: np.ascontiguousarray(np.asarray(a, dtype=np.float32))
    x_prompt, x_sample, c_prompt, c_sample = f(x_prompt), f(x_sample), f(c_prompt), f(c_sample)
    state_conv, state_ret = f(state_conv), f(state_ret)
    if "nc" not in _CACHE:
        _CACHE["nc"] = build_program()
    nc, hc = _CACHE["nc"]
    shared = {
        "ada_w": f(ada_w)[0], "ada_b": f(ada_b)[0], "norm_pre": f(norm_pre)[0], "norm_post": f(norm_post)[0],
        "w_in": f(w_in)[0], "conv_w": f(conv_w)[0], "conv_b": f(conv_b)[0], "w_branch": f(w_branch)[0], "w_out": f(w_out)[0],
    }
    for k in CONST_SHAPES:
        shared["k_" + k] = np.ascontiguousarray(hc[k])
    in_maps = []
    for i in range(N_CORES):
        m = dict(shared)
        m["xp"] = x_prompt[2 * i:2 * i + 2].reshape(2 * SEQ, D)
        m["xs"] = x_sample[2 * i:2 * i + 2].reshape(2 * DEC, D)
        m["c4"] = np.ascontiguousarray(np.concatenate([c_prompt[2 * i:2 * i + 2], c_sample[2 * i:2 * i + 2]], 0))
        m["sconv"] = np.ascontiguousarray(state_conv[0, 2 * i:2 * i + 2])
        m["sret"] = np.ascontiguousarray(state_ret[0, 2 * i:2 * i + 2])
        in_maps.append(m)
    res = run_bass_kernel_spmd(nc, in_maps, core_ids=list(range(N_CORES)))
    r = res.results
    y_prompt = np.concatenate([r[i]["yp"].reshape(2, SEQ, D) for i in range(N_CORES)], 0)
    y_sample = np.concatenate([r[i]["ys"].reshape(2, DEC, D) for i in range(N_CORES)], 0)
    ncp = np.concatenate([r[i]["ncp"] for i in range(N_CORES)], 0)[None]
    nrp = np.concatenate([r[i]["nrp"] for i in range(N_CORES)], 0)[None]
    ncs = np.concatenate([r[i]["ncs"] for i in range(N_CORES)], 0)[None]
    nrs = np.concatenate([r[i]["nrs"] for i in range(N_CORES)], 0)[None]
    return (y_prompt.astype(np.float32), y_sample.astype(np.float32), ncp.astype(np.float32), nrp.astype(np.float32),
            ncs.astype(np.float32), nrs.astype(np.float32))
```
